# Optimizing a Trainium2 kernel written in Bass

```python
import jax, jax.numpy as jnp
from jax import lax
import numpy as np

D_MODEL = 1024
BATCH = 8
SEQ = 4096
DEPTH = 1

GRID_W = 64
N_HEADS = 8
N_KV_HEADS = 2
HEAD_DIM = 64
GQA_GROUP = N_HEADS // N_KV_HEADS
ATTN_WIDTH = N_HEADS * HEAD_DIM
KV_WIDTH = N_KV_HEADS * HEAD_DIM
LRU_WIDTH = D_MODEL - ATTN_WIDTH
LRU_BLOCKS = 8
LRU_BLOCK = LRU_WIDTH // LRU_BLOCKS
LRU_C = 8.0
LRU_CONV_W = 4
D_FF = 3 * D_MODEL
FFN_CONV_W = 3
Q_BLOCK = 128
ROPE_THETA = 10000.0
EPS = 1e-6
IN_WIDTH = ATTN_WIDTH + 2 * KV_WIDTH + 2 * LRU_WIDTH

kernel_name = "hymba_attn_rglru_convffn_encoder"


def rms_norm(x, g):
    xf = x.astype(jnp.float32)
    y = xf * lax.rsqrt(jnp.mean(xf * xf, axis=-1, keepdims=True) + EPS)
    return (y * g.astype(jnp.float32)).astype(x.dtype)


def depthwise_conv(x, w, b, left, right):
    y = lax.conv_general_dilated(
        x, w[:, None, :].astype(x.dtype), window_strides=(1,), padding=[(left, right)],
        dimension_numbers=("NWC", "WIO", "NWC"), feature_group_count=x.shape[-1])
    return y + b.astype(x.dtype)


def rope_1d(x, pos):
    d = x.shape[-1]
    inv_freq = 1.0 / (ROPE_THETA ** (jnp.arange(0, d, 2, dtype=jnp.float32) / d))
    ang = pos.astype(jnp.float32)[:, None] * inv_freq[None, :]
    cos = jnp.cos(ang)[None, :, None, :]
    sin = jnp.sin(ang)[None, :, None, :]
    xf = x.astype(jnp.float32)
    x1, x2 = xf[..., : d // 2], xf[..., d // 2:]
    return jnp.concatenate([x1 * cos - x2 * sin, x2 * cos + x1 * sin], axis=-1).astype(x.dtype)


def axial_rope(x, row, col):
    half = x.shape[-1] // 2
    return jnp.concatenate([rope_1d(x[..., :half], row), rope_1d(x[..., half:], col)], axis=-1)


def block_attention(q, k, v):
    B, S = q.shape[0], q.shape[1]
    nb = S // Q_BLOCK
    qg = q.reshape(B, nb, Q_BLOCK, N_KV_HEADS, GQA_GROUP, HEAD_DIM).transpose(1, 0, 2, 3, 4, 5)

    def one_block(qb):
        s = jnp.einsum("bqkgd,bskd->bkgqs", qb, k).astype(jnp.float32)
        p = jax.nn.softmax(s, axis=-1).astype(v.dtype)
        return jnp.einsum("bkgqs,bskd->bqkgd", p, v)

    o = lax.map(one_block, qg)
    return o.transpose(1, 0, 2, 3, 4, 5).reshape(B, S, ATTN_WIDTH)


def _linear_combine(left, right):
    a1, b1 = left
    a2, b2 = right
    return a1 * a2, a2 * b1 + b2


def rg_lru(xc, wa, ba, wx, bx, lam, reverse):
    B, S, _ = xc.shape
    xf = xc.astype(jnp.float32)
    xb = xf.reshape(B, S, LRU_BLOCKS, LRU_BLOCK)
    r = jax.nn.sigmoid(jnp.einsum("bshi,hij->bshj", xb, wa.astype(jnp.float32)) + ba.astype(jnp.float32))
    i = jax.nn.sigmoid(jnp.einsum("bshi,hij->bshj", xb, wx.astype(jnp.float32)) + bx.astype(jnp.float32))
    r = r.reshape(B, S, LRU_WIDTH)
    i = i.reshape(B, S, LRU_WIDTH)
    log_a = LRU_C * r * jax.nn.log_sigmoid(lam.astype(jnp.float32))
    a = jnp.exp(log_a)
    mult = jnp.sqrt(-jnp.expm1(2.0 * log_a))
    start = S - 1 if reverse else 0
    mult = jnp.where((jnp.arange(S) == start)[None, :, None], 1.0, mult)
    b = mult * i * xf
    _, h = lax.associative_scan(_linear_combine, (a, b), reverse=reverse, axis=1)
    return h


def setup_inputs(seed: int = 0) -> dict:
    key = jax.random.key(seed)
    ks = jax.random.split(key, 32)
    f32 = jnp.float32

    def nrm(k, shape, scale):
        return jax.random.normal(k, shape, f32) * scale

    def gain(k, shape):
        return 1.0 + 0.02 * jax.random.normal(k, shape, f32)

    def lam_init(k):
        u = jax.random.uniform(k, (DEPTH, LRU_WIDTH), f32, 0.9, 0.999)
        a = u ** (1.0 / LRU_C)
        return jnp.log(a) - jnp.log1p(-a)

    L = DEPTH
    return {
        "x": jax.random.normal(ks[0], (BATCH, SEQ, D_MODEL), f32),
        "norm1_g": gain(ks[1], (L, D_MODEL)),
        "w_in": nrm(ks[2], (L, D_MODEL, IN_WIDTH), D_MODEL ** -0.5),
        "q_norm_g": gain(ks[3], (L, HEAD_DIM)),
        "k_norm_g": gain(ks[4], (L, HEAD_DIM)),
        "lru_conv_w": nrm(ks[5], (L, LRU_CONV_W, LRU_WIDTH), LRU_CONV_W ** -0.5),
        "lru_conv_b": nrm(ks[6], (L, LRU_WIDTH), 0.01),
        "wa_f": nrm(ks[7], (L, LRU_BLOCKS, LRU_BLOCK, LRU_BLOCK), LRU_BLOCK ** -0.5),
        "ba_f": nrm(ks[8], (L, LRU_BLOCKS, LRU_BLOCK), 0.01),
        "wx_f": nrm(ks[9], (L, LRU_BLOCKS, LRU_BLOCK, LRU_BLOCK), LRU_BLOCK ** -0.5),
        "bx_f": nrm(ks[10], (L, LRU_BLOCKS, LRU_BLOCK), 0.01),
        "lam_f": lam_init(ks[11]),
        "wa_b": nrm(ks[12], (L, LRU_BLOCKS, LRU_BLOCK, LRU_BLOCK), LRU_BLOCK ** -0.5),
        "ba_b": nrm(ks[13], (L, LRU_BLOCKS, LRU_BLOCK), 0.01),
        "wx_b": nrm(ks[14], (L, LRU_BLOCKS, LRU_BLOCK, LRU_BLOCK), LRU_BLOCK ** -0.5),
        "bx_b": nrm(ks[15], (L, LRU_BLOCKS, LRU_BLOCK), 0.01),
        "lam_b": lam_init(ks[16]),
        "w_out": nrm(ks[17], (L, D_MODEL, D_MODEL), D_MODEL ** -0.5),
        "norm2_g": gain(ks[18], (L, D_MODEL)),
        "w_up": nrm(ks[19], (L, D_MODEL, 2 * D_FF), D_MODEL ** -0.5),
        "up_conv_w": nrm(ks[20], (L, FFN_CONV_W, 2 * D_FF), FFN_CONV_W ** -0.5),
        "up_conv_b": nrm(ks[21], (L, 2 * D_FF), 0.01),
        "w_down": nrm(ks[22], (L, D_FF, D_MODEL), D_FF ** -0.5),
        "final_g": gain(ks[23], (D_MODEL,)),
    }


def reference(x, norm1_g, w_in, q_norm_g, k_norm_g, lru_conv_w, lru_conv_b,
              wa_f, ba_f, wx_f, bx_f, lam_f, wa_b, ba_b, wx_b, bx_b, lam_b,
              w_out, norm2_g, w_up, up_conv_w, up_conv_b, w_down, final_g):
    B, S, _ = x.shape
    rows = S // GRID_W
    row = jnp.repeat(jnp.arange(rows, dtype=jnp.int32), GRID_W)
    col = jnp.tile(jnp.arange(GRID_W, dtype=jnp.int32), rows)
    splits = [ATTN_WIDTH, ATTN_WIDTH + KV_WIDTH, ATTN_WIDTH + 2 * KV_WIDTH,
              ATTN_WIDTH + 2 * KV_WIDTH + LRU_WIDTH]

    for l in range(DEPTH):
        h = rms_norm(x, norm1_g[l])
        z = h @ w_in[l]
        q, k, v, xr, yr = jnp.split(z, splits, axis=-1)

        q = rms_norm(q.reshape(B, S, N_HEADS, HEAD_DIM), q_norm_g[l])
        k = rms_norm(k.reshape(B, S, N_KV_HEADS, HEAD_DIM), k_norm_g[l])
        q = axial_rope(q, row, col) * (HEAD_DIM ** -0.5)
        k = axial_rope(k, row, col)
        v = v.reshape(B, S, N_KV_HEADS, HEAD_DIM)
        attn_out = block_attention(q, k, v)

        xc = depthwise_conv(xr, lru_conv_w[l], lru_conv_b[l], 2, 1)
        h_fwd = rg_lru(xc, wa_f[l], ba_f[l], wx_f[l], bx_f[l], lam_f[l], False)
        h_bwd = rg_lru(xc, wa_b[l], ba_b[l], wx_b[l], bx_b[l], lam_b[l], True)
        lru_out = ((h_fwd + h_bwd) * jax.nn.gelu(yr.astype(jnp.float32))).astype(x.dtype)

        mixed = jnp.concatenate([attn_out, lru_out], axis=-1) @ w_out[l]
        x = x + mixed

        h = rms_norm(x, norm2_g[l])
        u = depthwise_conv(h @ w_up[l], up_conv_w[l], up_conv_b[l], 1, 1)
        gate, val = jnp.split(u, 2, axis=-1)
        x = x + (jax.nn.gelu(gate) * val) @ w_down[l]

    return rms_norm(x, final_g)
```

```python
import numpy as np
from contextlib import ExitStack
import concourse.bass as bass
import concourse.mybir as mybir
from concourse.bass_utils import run_bass_kernel_spmd

F32 = mybir.dt.float32
BF16 = mybir.dt.bfloat16
AF = mybir.ActivationFunctionType
ALU = mybir.AluOpType

T = 4096
D = 1024
EPS = 1e-6
NWA = 8
XWA = 516
SC = 456
NWC = 9
WCMAX = SC + 2
GC = 0.7978845608028654
PV_G1, PV_G2, PV_GF, PV_GQ, PV_GK = 0, 8, 16, 24, 25
PV_LCW, PV_LCB = 26, 42
PV_BAF, PV_BXF, PV_BAB, PV_BXB = 46, 50, 54, 58
PV_LAMF, PV_LAMB = 62, 66
PV_UCW, PV_UCB = 70, 214
PV_N = 262
DV_HBAF, DV_HBXF, DV_HBAB, DV_HBXB = 0, 4, 8, 12
DV_CLF, DV_CLB, DV_HCLF, DV_HCLB = 16, 20, 24, 28
DV_N = 32
ITEMS = [("wq", j, 1024) for j in range(4)] + [("wo", m, 1024) for m in range(8)] + \
        [("wu", j, 2048) for j in range(24)] + [("wd", m, 3072) for m in range(8)]
ITEM_OFF = []
_o = 0
for _it in ITEMS:
    ITEM_OFF.append(_o)
    _o += _it[2]
SCR_W = _o
NSLOT = 4
SLOT_W = 3072


class Tl:
    __slots__ = ("t", "w", "r", "sem", "cnt", "name")

    def __init__(self, t, name):
        self.t = t
        self.w = None
        self.r = []
        self.sem = None
        self.cnt = 0
        self.name = name

    def __getitem__(self, k):
        return self.t[k]


class Ctx:
    def __init__(self, nc, es):
        self.nc = nc
        self.es = es
        self.eng = {"pe": nc.tensor, "act": nc.scalar, "dve": nc.vector, "pool": nc.gpsimd, "sp": nc.sync}
        self.sem = {k: es.enter_context(nc.semaphore("s_" + k)) for k in ("pe", "act", "dve", "pool")}
        self.cnt = {k: 0 for k in ("pe", "act", "dve", "pool")}
        self.seen = {k: {} for k in self.eng}
        self.nsem = 0

    def tile(self, es, name, shape, dt):
        return Tl(es.enter_context(self.nc.sbuf_tensor("t_" + name, shape, dt)), name)

    def ptile(self, es, name, shape, dt=F32):
        return Tl(es.enter_context(self.nc.psum_tensor("t_" + name, shape, dt)), name)

    def dram(self, name):
        return Tl(None, name)

    def _dsem(self, tl):
        if tl.sem is None:
            tl.sem = self.es.enter_context(self.nc.semaphore("d%d_%s" % (self.nsem, tl.name)))
            self.nsem += 1
        return tl.sem

    def _wait(self, e, deps):
        seen = self.seen[e]
        need = {}
        for d in deps:
            if d is None:
                continue
            if d[0] == "dma":
                key, val = ("dma", id(d[1])), d[2]
                semh = d[1]
            else:
                if d[0] == e and e in ("pe", "sp"):
                    continue
                key, val = d[0], d[1]
                semh = self.sem[d[0]]
            if seen.get(key, 0) >= val:
                continue
            if key not in need or need[key][1] < val:
                need[key] = (semh, val)
        for key, (semh, val) in need.items():
            self.eng[e].wait_ge(semh, val)
            seen[key] = val

    def _deps(self, reads, writes):
        deps = []
        for t in reads:
            deps.append(t.w)
        for t in writes:
            deps.append(t.w)
            deps.extend(t.r)
        return deps

    def op(self, e, fn, reads=(), writes=()):
        self._wait(e, self._deps(reads, writes))
        ins = fn()
        self.cnt[e] += 1
        ins.then_inc(self.sem[e], 1)
        tag = (e, self.cnt[e])
        for t in writes:
            t.w = tag
            t.r = []
        for t in reads:
            if t.w is not None and t.w == tag:
                continue
            t.r.append(tag)
        return ins

    def dma(self, q, out_ap, in_ap, reads=(), writes=(), semtile=None):
        self._wait(q, self._deps(reads, writes))
        st = semtile if semtile is not None else (writes[0] if writes else reads[0])
        semh = self._dsem(st)
        st.cnt += 16
        self.eng[q].dma_start(out=out_ap, in_=in_ap).then_inc(semh, 16)
        tag = ("dma", semh, st.cnt)
        for t in writes:
            t.w = tag
            t.r = []
        for t in reads:
            t.r.append(tag)

    def barrier(self, dma_tiles=()):
        for e in ("pe", "act", "dve", "pool", "sp"):
            deps = [(o, self.cnt[o]) for o in ("pe", "act", "dve", "pool") if self.cnt[o] > 0]
            for t in dma_tiles:
                if t.sem is not None:
                    deps.append(("dma", t.sem, t.cnt))
            self._wait(e, deps)


def _build(debug=False, nwc=NWC):
    nc = bass.Bass("TRN2", target_bir_lowering=False)
    dt = nc.dram_tensor
    x_d = dt("xT", [D, T + 4], F32, kind="ExternalInput").ap()
    wq_d = dt("wq", [D, 512], F32, kind="ExternalInput").ap()
    wkvl_d = dt("wkvl", [D, 1280], F32, kind="ExternalInput").ap()
    wo_d = dt("wo", [D, D], F32, kind="ExternalInput").ap()
    wu_d = dt("wu", [D, 6144], F32, kind="ExternalInput").ap()
    wd_d = dt("wd", [3072, D], F32, kind="ExternalInput").ap()
    pvec_d = dt("pvec", [128, PV_N], F32, kind="ExternalInput").ap()
    gatew_d = dt("gatew", [128, 16 * 128], F32, kind="ExternalInput").ap()
    cmat_d = dt("cmat", [128, 4 * 128], F32, kind="ExternalInput").ap()
    tab_d = dt("tab", [128, 2, T + 4], F32, kind="ExternalInput").ap()
    out_d = dt("outT", [D, T], F32, kind="ExternalOutput").ap()
    scr_d = dt("wscr", [128, SCR_W], BF16).ap()
    dbg = {}
    if debug:
        dbg["k"] = dt("dbg_k", [128, T], BF16, kind="ExternalOutput").ap()
        dbg["v"] = dt("dbg_v", [128, 32 * 128], BF16, kind="ExternalOutput").ap()
        dbg["lru"] = dt("dbg_lru", [128, 4 * (T + 2)], BF16, kind="ExternalOutput").ap()

    xv = x_d.rearrange("(c p) t -> p c t", p=128)
    ov = out_d.rearrange("(c p) t -> p c t", p=128)

    with ExitStack() as es:
        C = Ctx(nc, es)
        op, dma = C.op, C.dma
        V = nc.vector
        A = nc.scalar
        G = nc.gpsimd
        PE = nc.tensor

        pvec = C.tile(es, "pvec", [128, PV_N], F32)
        dv = C.tile(es, "dv", [128, DV_N], F32)
        cmat = C.tile(es, "cmat", [128, 512], F32)
        cbf = C.tile(es, "cbf", [128, 512], BF16)
        gw = C.tile(es, "gw", [128, 16 * 128], BF16)
        kT = C.tile(es, "kT", [128, T], BF16)
        vt = C.tile(es, "vt", [128, 32, 128], BF16)
        lru = C.tile(es, "lru", [128, 4, T + 2], BF16)
        carry = C.tile(es, "carry", [128, 4], F32)
        S = [C.ptile(es, "S%d" % i, [128, 2, 512]) for i in range(2)]
        P = [C.ptile(es, "P%d" % i, [128, 512]) for i in range(4)]
        scr = C.dram("scr")
        ones_bf = cbf[:, 0:128]
        bd_bf = cbf[:, 128:256]
        ident_bf = cbf[:, 384:512]
        R32 = cmat[:, 256:384]

        dma("sp", pvec[:, :], pvec_d[:, :], writes=[pvec])
        dma("sp", cmat[:, :], cmat_d[:, :], writes=[cmat])
        op("pool", lambda: G.tensor_copy(out=cbf[:, :], in_=cmat[:, :]), reads=[cmat], writes=[cbf])
        op("pool", lambda: G.memset(lru[:, :, :], 0.0), writes=[lru])
        op("pool", lambda: G.memset(carry[:, :], 0.0), writes=[carry])
        for (src, dst) in ((PV_BAF, DV_HBAF), (PV_BXF, DV_HBXF), (PV_BAB, DV_HBAB), (PV_BXB, DV_HBXB)):
            op("dve", lambda src=src, dst=dst: V.tensor_scalar(
                out=dv[:, dst:dst + 4], in0=pvec[:, src:src + 4], scalar1=0.5, scalar2=None, op0=ALU.mult),
               reads=[pvec], writes=[dv])
        with ExitStack() as es0:
            et0 = C.tile(es0, "s_e", [128, 8], F32)
            e2 = C.tile(es0, "s_e2", [128, 8], F32)
            acc = C.tile(es0, "s_acc", [128, 8], F32)
            op("act", lambda: A.activation(out=et0[:, :], in_=pvec[:, PV_LAMF:PV_LAMF + 8], func=AF.Exp, scale=-1.0),
               reads=[pvec], writes=[et0])
            op("dve", lambda: V.tensor_scalar(out=acc[:, :], in0=et0[:, :], scalar1=-0.25, scalar2=1.0 / 3.0,
                                              op0=ALU.mult, op1=ALU.add), reads=[et0], writes=[acc])
            op("dve", lambda: V.tensor_tensor(out=acc[:, :], in0=acc[:, :], in1=et0[:, :], op=ALU.mult),
               reads=[et0, acc], writes=[acc])
            op("dve", lambda: V.tensor_scalar(out=acc[:, :], in0=acc[:, :], scalar1=-0.5, scalar2=None, op0=ALU.add),
               reads=[acc], writes=[acc])
            op("dve", lambda: V.tensor_tensor(out=acc[:, :], in0=acc[:, :], in1=et0[:, :], op=ALU.mult),
               reads=[et0, acc], writes=[acc])
            op("dve", lambda: V.tensor_scalar(out=acc[:, :], in0=acc[:, :], scalar1=1.0, scalar2=None, op0=ALU.add),
               reads=[acc], writes=[acc])
            op("dve", lambda: V.tensor_tensor(out=e2[:, :], in0=acc[:, :], in1=et0[:, :], op=ALU.mult),
               reads=[et0, acc], writes=[e2])
            op("dve", lambda: V.tensor_scalar(out=dv[:, DV_CLF:DV_CLF + 8], in0=e2[:, :], scalar1=-8.0, scalar2=None,
                                              op0=ALU.mult), reads=[e2], writes=[dv])
            op("dve", lambda: V.tensor_scalar(out=dv[:, DV_HCLF:DV_HCLF + 8], in0=e2[:, :], scalar1=-4.0, scalar2=None,
                                              op0=ALU.mult), reads=[e2], writes=[dv])
            C.barrier()

        with ExitStack() as esA:
            wk = C.tile(esA, "wkvl", [128, 8, 1280], BF16)
            dma("pool", gw[:, :], gatew_d[:, :], writes=[gw])
            wkv = wkvl_d.rearrange("(c p) n -> p c n", p=128)
            for c in range(8):
                dma("pool", wk[:, c, :], wkv[:, c, :], writes=[wk])
            wqv = wq_d.rearrange("(c p) n -> p c n", p=128)
            wov = wo_d.rearrange("(c p) n -> p c n", p=128)
            wuv = wu_d.rearrange("(c p) n -> p c n", p=128)
            wdv = wd_d.rearrange("(c p) n -> p c n", p=128)
            scr_sem = C._dsem(scr)

            def cast(dst, src):
                nc.gpsimd.dma_start(out=dst, in_=src).then_inc(scr_sem, 16)
                scr.cnt += 16
            for it, off in zip(ITEMS, ITEM_OFF):
                kind, j, w = it
                if kind == "wq":
                    cast(scr_d[:, off:off + w].rearrange("p (c n) -> p c n", c=8), wqv[:, :, j * 128:(j + 1) * 128])
                elif kind == "wo":
                    cast(scr_d[:, off:off + w].rearrange("p (c n) -> p c n", c=8), wov[:, :, j * 128:(j + 1) * 128])
                elif kind == "wu":
                    cast(scr_d[:, off:off + 1024].rearrange("p (c n) -> p c n", c=8), wuv[:, :, j * 128:(j + 1) * 128])
                    cast(scr_d[:, off + 1024:off + 2048].rearrange("p (c n) -> p c n", c=8),
                         wuv[:, :, 3072 + j * 128:3072 + (j + 1) * 128])
                else:
                    cast(scr_d[:, off:off + w].rearrange("p (c n) -> p c n", c=24), wdv[:, :, j * 128:(j + 1) * 128])
            scr.w = ("dma", scr_sem, scr.cnt)

            xw = [C.tile(esA, "xwA%d" % i, [128, 8, XWA], F32) for i in range(2)]
            sq = C.tile(esA, "sqA", [128, 8, XWA], BF16)
            xb = C.tile(esA, "xbA", [128, 8, XWA], BF16)
            sd = C.tile(esA, "sdA", [128, XWA], F32)
            rs = C.tile(esA, "rsA", [128, XWA], F32)
            xr = C.tile(esA, "xrA", [128, 4, XWA], F32)
            tb = C.tile(esA, "tabA", [128, 2, 512], F32)
            gen = [C.tile(esA, "gen%d" % i, [128, 512], F32) for i in range(5)]
            ksq = C.tile(esA, "ksq", [128, 512], BF16)
            vzb = C.tile(esA, "vzb", [128, 512], BF16)
            XC = [C.tile(esA, "xc%d" % i, [128, 512], F32) for i in range(4)]
            UU = [C.tile(esA, "uu%d" % i, [128, 512], F32) for i in range(4)]
            A2 = [C.tile(esA, "a2%d" % i, [128, 512], F32) for i in range(4)]
            AA = [C.tile(esA, "aa%d" % i, [128, 512], F32) for i in range(4)]
            TR = [C.tile(esA, "tr%d" % i, [128, 512], F32) for i in range(2)]
            HH = [C.tile(esA, "hh%d" % i, [128, 512], F32) for i in range(2)]
            XCB = [C.tile(esA, "xcb%d" % i, [128, 512], BF16) for i in range(2)]

            def load_xA(i, slot):
                s = i * 512
                dma("sp", xw[slot][:, :, 0:515], xv[:, :, s:s + 515], writes=[xw[slot]])

            def passA(fwd):
                order = list(range(NWA)) if fwd else list(range(NWA - 1, -1, -1))
                load_xA(order[0], 0)
                kw = 0 if fwd else 2
                hba = DV_HBAF if fwd else DV_HBAB
                hbx = DV_HBXF if fwd else DV_HBXB
                cl = DV_CLF if fwd else DV_CLB
                hcl = DV_HCLF if fwd else DV_HCLB
                for n, i in enumerate(order):
                    slot = n % 2
                    s = i * 512
                    X = xw[slot]
                    if n + 1 < len(order):
                        load_xA(order[n + 1], (n + 1) % 2)
                    if fwd:
                        dma("sp", tb[:, :, :], tab_d[:, :, s + 2:s + 514], writes=[tb])
                    op("act", lambda: A.activation(out=sq[:, :, 0:515], in_=X[:, :, 0:515], func=AF.Square),
                       reads=[X], writes=[sq])

                    def ss_mm():
                        for c in range(8):
                            PE.matmul(P[0][:, 0:512], lhsT=ones_bf, rhs=sq[:, c, 0:512], start=(c == 0), stop=(c == 7))
                        for c in range(8):
                            last = PE.matmul(P[1][:, 0:4], lhsT=ones_bf, rhs=sq[:, c, 511:515], start=(c == 0), stop=(c == 7))
                        return last
                    op("pe", ss_mm, reads=[sq, cbf], writes=[P[0], P[1]])
                    op("act", lambda: A.activation(out=sd[:, 0:512], in_=P[0][:, 0:512], func=AF.Sqrt, scale=1.0 / D, bias=EPS),
                       reads=[P[0]], writes=[sd])
                    op("act", lambda: A.activation(out=sd[:, 512:515], in_=P[1][:, 1:4], func=AF.Sqrt, scale=1.0 / D, bias=EPS),
                       reads=[P[1], sd], writes=[sd])
                    op("dve", lambda: V.reciprocal(out=rs[:, 0:515], in_=sd[:, 0:515]), reads=[sd], writes=[rs])
                    for c in range(8):
                        op("pool", lambda c=c: G.tensor_scalar(out=xb[:, c, 0:515], in0=X[:, c, 0:515],
                                                                scalar1=pvec[:, PV_G1 + c:PV_G1 + c + 1], scalar2=0.0,
                                                                op0=ALU.mult, op1=ALU.add),
                           reads=[X, pvec], writes=[xb])
                    for c in range(4):
                        col0 = 256 + c * 128
                        pm = P[2 + (c % 2)]

                        def xr_mm(col0=col0, pm=pm):
                            for kc in range(8):
                                last = PE.matmul(pm[:, 0:512], lhsT=wk[:, kc, col0:col0 + 128], rhs=xb[:, kc, 0:512],
                                                 start=(kc == 0), stop=(kc == 7))
                            return last
                        op("pe", xr_mm, reads=[wk, xb], writes=[pm])
                        op("dve", lambda c=c, pm=pm: V.tensor_tensor(out=xr[:, c, 0:512], in0=pm[:, 0:512], in1=rs[:, 0:512],
                                                                     op=ALU.mult), reads=[pm, rs], writes=[xr])

                    def xrh_mm():
                        for c in range(4):
                            col0 = 256 + c * 128
                            for kc in range(8):
                                last = PE.matmul(P[1][:, 16 + c * 4:16 + c * 4 + 4], lhsT=wk[:, kc, col0:col0 + 128],
                                                 rhs=xb[:, kc, 511:515], start=(kc == 0), stop=(kc == 7), skip_group_check=True)
                        return last
                    op("pe", xrh_mm, reads=[wk, xb], writes=[P[1]])
                    for c in range(4):
                        op("dve", lambda c=c: V.tensor_tensor(out=xr[:, c, 512:515], in0=P[1][:, 16 + c * 4 + 1:16 + c * 4 + 4],
                                                              in1=rs[:, 512:515], op=ALU.mult), reads=[P[1], rs], writes=[xr])
                    if fwd:
                        kz, ksd, kn, kt1, kt2 = gen

                        def k_mm():
                            for kc in range(8):
                                last = PE.matmul(P[2][:, 0:512], lhsT=wk[:, kc, 0:128], rhs=xb[:, kc, 2:514],
                                                 start=(kc == 0), stop=(kc == 7))
                            return last
                        op("pe", k_mm, reads=[wk, xb], writes=[P[2]])
                        op("dve", lambda: V.tensor_tensor(out=kz[:, :], in0=P[2][:, :], in1=rs[:, 2:514], op=ALU.mult),
                           reads=[P[2], rs], writes=[kz])
                        op("act", lambda: A.activation(out=ksq[:, :], in_=kz[:, :], func=AF.Square), reads=[kz], writes=[ksq])
                        op("pe", lambda: PE.matmul(P[2][:, :], lhsT=bd_bf, rhs=ksq[:, :], start=True, stop=True),
                           reads=[cbf, ksq], writes=[P[2]])
                        op("act", lambda: A.activation(out=ksd[:, :], in_=P[2][:, :], func=AF.Sqrt, scale=1.0 / 64, bias=EPS),
                           reads=[P[2]], writes=[ksd])
                        op("dve", lambda: V.reciprocal(out=ksd[:, :], in_=ksd[:, :]), reads=[ksd], writes=[ksd])
                        op("dve", lambda: V.scalar_tensor_tensor(out=kn[:, :], in0=kz[:, :], scalar=pvec[:, PV_GK:PV_GK + 1],
                                                                 in1=ksd[:, :], op0=ALU.mult, op1=ALU.mult),
                           reads=[kz, ksd, pvec], writes=[kn])
                        op("pe", lambda: PE.matmul(P[2][:, :], lhsT=R32, rhs=kn[:, :], start=True, stop=True),
                           reads=[cmat, kn], writes=[P[2]])
                        op("pool", lambda: G.tensor_tensor(out=kt1[:, :], in0=kn[:, :], in1=tb[:, 0, :], op=ALU.mult),
                           reads=[kn, tb], writes=[kt1])
                        op("dve", lambda: V.tensor_tensor(out=kt2[:, :], in0=P[2][:, :], in1=tb[:, 1, :], op=ALU.mult),
                           reads=[P[2], tb], writes=[kt2])
                        op("dve", lambda: V.tensor_tensor(out=kT[:, s:s + 512], in0=kt1[:, :], in1=kt2[:, :], op=ALU.add),
                           reads=[kt1, kt2], writes=[kT])

                        def v_mm():
                            for kc in range(8):
                                last = PE.matmul(P[3][:, 0:512], lhsT=wk[:, kc, 128:256], rhs=xb[:, kc, 2:514],
                                                 start=(kc == 0), stop=(kc == 7))
                            return last
                        op("pe", v_mm, reads=[wk, xb], writes=[P[3]])
                        op("dve", lambda: V.tensor_tensor(out=vzb[:, :], in0=P[3][:, :], in1=rs[:, 2:514], op=ALU.mult),
                           reads=[P[3], rs], writes=[vzb])
                        pvb = P[3][:, 0:256].bitcast(BF16)

                        def v_tr():
                            for q in range(4):
                                last = PE.transpose(out=pvb[:, q * 128:(q + 1) * 128], in_=vzb[:, q * 128:(q + 1) * 128],
                                                    identity=ident_bf)
                            return last
                        op("pe", v_tr, reads=[vzb, cbf], writes=[P[3]])
                        op("act", lambda: A.activation(out=vt[:, i * 4:i * 4 + 4, :], in_=pvb.rearrange("p (q n) -> p q n", q=4),
                                                       func=AF.Copy), reads=[P[3]], writes=[vt])
                    for c in range(4):
                        w0 = PV_LCW + c * 4
                        op("act", lambda c=c, w0=w0: A.activation(out=XC[c][:, :], in_=xr[:, c, 0:512], func=AF.Identity,
                                                                  scale=pvec[:, w0:w0 + 1], bias=pvec[:, PV_LCB + c:PV_LCB + c + 1]),
                           reads=[xr, pvec], writes=[XC[c]])
                        for j in range(1, 4):
                            op("dve", lambda c=c, j=j, w0=w0: V.scalar_tensor_tensor(
                                out=XC[c][:, :], in0=xr[:, c, j:j + 512], scalar=pvec[:, w0 + j:w0 + j + 1], in1=XC[c][:, :],
                                op0=ALU.mult, op1=ALU.add), reads=[xr, pvec, XC[c]], writes=[XC[c]])
                    for c in range(4):
                        xcb = XCB[c % 2]
                        trc = TR[c % 2]
                        Sg = S[c % 2]
                        op("pool", lambda c=c, xcb=xcb: G.tensor_copy(out=xcb[:, :], in_=XC[c][:, :]), reads=[XC[c]], writes=[xcb])

                        def g_mm(c=c, xcb=xcb, Sg=Sg):
                            PE.matmul(Sg[:, 0, :], lhsT=gw[:, (kw * 4 + c) * 128:(kw * 4 + c + 1) * 128], rhs=xcb[:, :],
                                      start=True, stop=True)
                            return PE.matmul(Sg[:, 1, :], lhsT=gw[:, ((kw + 1) * 4 + c) * 128:((kw + 1) * 4 + c + 1) * 128],
                                             rhs=xcb[:, :], start=True, stop=True)
                        op("pe", g_mm, reads=[gw, xcb], writes=[Sg])
                        op("act", lambda c=c, trc=trc, Sg=Sg: A.activation(out=trc[:, :], in_=Sg[:, 0, :], func=AF.Tanh, scale=0.5,
                                                                           bias=dv[:, hba + c:hba + c + 1]), reads=[Sg, dv], writes=[trc])
                        op("act", lambda c=c, Sg=Sg: A.activation(out=UU[c][:, :], in_=Sg[:, 1, :], func=AF.Tanh, scale=0.5,
                                                                  bias=dv[:, hbx + c:hbx + c + 1]), reads=[Sg, dv], writes=[UU[c]])
                        op("act", lambda c=c, trc=trc: A.activation(out=AA[c][:, :], in_=trc[:, :], func=AF.Exp,
                                                                    scale=dv[:, hcl + c:hcl + c + 1], bias=dv[:, hcl + c:hcl + c + 1]),
                           reads=[trc, dv], writes=[AA[c]])
                        op("act", lambda c=c, trc=trc: A.activation(out=A2[c][:, :], in_=trc[:, :], func=AF.Exp,
                                                                    scale=dv[:, cl + c:cl + c + 1], bias=dv[:, cl + c:cl + c + 1]),
                           reads=[trc, dv], writes=[A2[c]])
                        op("dve", lambda c=c: V.scalar_tensor_tensor(out=UU[c][:, :], in0=UU[c][:, :], scalar=1.0, in1=XC[c][:, :],
                                                                     op0=ALU.add, op1=ALU.mult), reads=[UU[c], XC[c]], writes=[UU[c]])
                    if not fwd:
                        Y1 = [gen[0], gen[1]]
                        Y2 = [gen[2], gen[3]]
                    for c in range(4):
                        op("act", lambda c=c: A.activation(out=A2[c][:, :], in_=A2[c][:, :], func=AF.Sqrt, scale=-1.0, bias=1.0),
                           reads=[A2[c]], writes=[A2[c]])
                    for c in range(4):
                        hh = HH[c % 2]
                        first = (fwd and i == 0) or ((not fwd) and i == NWA - 1)
                        if first:
                            col = 0 if fwd else 511
                            op("pool", lambda c=c, col=col: G.memset(A2[c][:, col:col + 1], 1.0), writes=[A2[c]])
                        op("dve", lambda c=c: V.scalar_tensor_tensor(out=UU[c][:, :], in0=UU[c][:, :], scalar=0.5, in1=A2[c][:, :],
                                                                     op0=ALU.mult, op1=ALU.mult), reads=[UU[c], A2[c]], writes=[UU[c]])
                        if fwd:
                            op("dve", lambda c=c, hh=hh: V.tensor_tensor_scan(out=hh[:, :], data0=AA[c][:, :], data1=UU[c][:, :],
                                                                              initial=carry[:, c:c + 1], op0=ALU.mult, op1=ALU.add),
                               reads=[AA[c], UU[c], carry], writes=[hh])
                            op("pool", lambda c=c, hh=hh: G.tensor_copy(out=carry[:, c:c + 1], in_=hh[:, 511:512]),
                               reads=[hh], writes=[carry])
                            op("pool", lambda c=c, hh=hh: G.tensor_copy(out=lru[:, c, 1 + s:1 + s + 512], in_=hh[:, :]),
                               reads=[hh], writes=[lru])
                        else:
                            op("dve", lambda c=c, hh=hh: V.tensor_tensor_scan(out=hh[:, ::-1], data0=AA[c][:, ::-1],
                                                                              data1=UU[c][:, ::-1], initial=carry[:, c:c + 1],
                                                                              op0=ALU.mult, op1=ALU.add),
                               reads=[AA[c], UU[c], carry], writes=[hh])
                            op("pool", lambda c=c, hh=hh: G.tensor_copy(out=carry[:, c:c + 1], in_=hh[:, 0:1]),
                               reads=[hh], writes=[carry])
                            y1, y2 = Y1[c % 2], Y2[c % 2]
                            pm = P[2 + (c % 2)]
                            col0 = 768 + c * 128

                            def y_mm(col0=col0, pm=pm):
                                for kc in range(8):
                                    last = PE.matmul(pm[:, 0:512], lhsT=wk[:, kc, col0:col0 + 128], rhs=xb[:, kc, 2:514],
                                                     start=(kc == 0), stop=(kc == 7))
                                return last
                            op("pe", y_mm, reads=[wk, xb], writes=[pm])
                            op("dve", lambda y1=y1, pm=pm: V.tensor_tensor(out=y1[:, :], in0=pm[:, :], in1=rs[:, 2:514], op=ALU.mult),
                               reads=[pm, rs], writes=[y1])
                            op("pool", lambda y1=y1, y2=y2: G.tensor_tensor(out=y2[:, :], in0=y1[:, :], in1=y1[:, :], op=ALU.mult),
                               reads=[y1], writes=[y2])
                            op("pool", lambda y2=y2: G.tensor_scalar(out=y2[:, :], in0=y2[:, :], scalar1=0.044715, scalar2=1.0,
                                                                     op0=ALU.mult, op1=ALU.add), reads=[y2], writes=[y2])
                            op("pool", lambda y1=y1, y2=y2: G.tensor_tensor(out=y2[:, :], in0=y2[:, :], in1=y1[:, :], op=ALU.mult),
                               reads=[y1, y2], writes=[y2])
                            op("act", lambda y2=y2: A.activation(out=y2[:, :], in_=y2[:, :], func=AF.Tanh, scale=GC),
                               reads=[y2], writes=[y2])
                            op("dve", lambda y1=y1, y2=y2: V.scalar_tensor_tensor(out=y1[:, :], in0=y2[:, :], scalar=1.0, in1=y1[:, :],
                                                                                  op0=ALU.add, op1=ALU.mult), reads=[y1, y2], writes=[y1])
                            op("pool", lambda c=c, hh=hh: G.tensor_tensor(out=hh[:, :], in0=hh[:, :], in1=lru[:, c, 1 + s:1 + s + 512],
                                                                          op=ALU.add), reads=[hh, lru], writes=[hh])
                            op("dve", lambda c=c, hh=hh, y1=y1: V.scalar_tensor_tensor(out=lru[:, c, 1 + s:1 + s + 512], in0=hh[:, :],
                                                                                       scalar=0.5, in1=y1[:, :], op0=ALU.mult,
                                                                                       op1=ALU.mult), reads=[hh, y1, lru], writes=[lru])

            passA(True)
            op("pool", lambda: G.memset(carry[:, :], 0.0), writes=[carry])
            passA(False)
            if debug:
                dma("sp", dbg["k"][:, :], kT[:, :], reads=[kT])
                dma("sp", dbg["v"][:, :], vt[:, :, :].rearrange("p a b -> p (a b)"), reads=[vt])
                dma("sp", dbg["lru"][:, :], lru[:, :, :].rearrange("p a b -> p (a b)"), reads=[lru])
            C.barrier(dma_tiles=[kT, vt, lru, scr] + xw)

        with ExitStack() as esC:
            ring = [C.tile(esC, "ring%d" % i, [128, SLOT_W], BF16) for i in range(NSLOT)]
            xw = [C.tile(esC, "xwC%d" % i, [128, 8, WCMAX], F32) for i in range(2)]
            sqF = C.tile(esC, "sqF", [128, 8, WCMAX], BF16)
            sdF = C.tile(esC, "sdF", [128, WCMAX], F32)
            rsF = C.tile(esC, "rsF", [128, WCMAX], F32)
            sqB = C.tile(esC, "sqB", [128, 8, WCMAX], BF16)
            sdB = C.tile(esC, "sdB", [128, WCMAX], F32)
            rsB = C.tile(esC, "rsB", [128, WCMAX], F32)
            xb = C.tile(esC, "xbC", [128, 8, WCMAX], BF16)
            h2 = C.tile(esC, "h2C", [128, 8, WCMAX], BF16)
            tbw = C.tile(esC, "tabC", [128, 2, WCMAX], F32)
            qT = C.tile(esC, "qT", [128, 4, WCMAX], BF16)
            attn2 = [C.tile(esC, "attn%d" % i, [128, 4, WCMAX], BF16) for i in range(2)]
            qz = C.tile(esC, "qz", [128, WCMAX], F32)
            rden = qz
            qsd = C.tile(esC, "qsd", [128, WCMAX], F32)
            qn = C.tile(esC, "qn", [128, WCMAX], F32)
            qt1 = C.tile(esC, "qt1", [128, WCMAX], F32)
            qt2 = C.tile(esC, "qt2", [128, WCMAX], F32)
            qsq = C.tile(esC, "qsq", [128, WCMAX], BF16)
            ET = [C.tile(esC, "et%d" % i, [128, 2, WCMAX], BF16) for i in range(3)]
            aall = C.tile(esC, "aall", [128, 24, SC], BF16)
            TG = [C.tile(esC, "tg%d" % i, [128, SC], F32) for i in range(2)]
            TV = [C.tile(esC, "tv%d" % i, [128, SC], F32) for i in range(2)]
            FS = [C.tile(esC, "fs%d" % i, [128, SC], F32) for i in range(2)]
            SF = [S[0], S[1]]
            po, pden = P[0], P[1]
            PB = [P[2], P[3]]

            nitems = len(ITEMS)
            item_idx = {(k, j): n for n, (k, j, w) in enumerate(ITEMS)}
            st = {"issued": 0, "used": 0, "seq": [], "dry": True}

            def issue_upto(n):
                seq = st["seq"]
                while st["issued"] < min(n, len(seq)):
                    g = st["issued"]
                    it = ITEMS[seq[g]]
                    off = ITEM_OFF[seq[g]]
                    slot = ring[g % NSLOT]
                    dma("sp", slot[:, 0:it[2]], scr_d[:, off:off + it[2]], reads=[scr], writes=[slot], semtile=slot)
                    st["issued"] += 1

            def get_item(kind, j):
                if st["dry"]:
                    st["seq"].append(item_idx[(kind, j)])
                    return ring[0]
                g = st["used"]
                assert st["seq"][g] == item_idx[(kind, j)]
                issue_upto(g + NSLOT)
                st["used"] += 1
                return ring[g % NSLOT]

            def dop(e, fn, reads=(), writes=()):
                if not st["dry"]:
                    op(e, fn, reads, writes)

            def ddma(*a, **k):
                if not st["dry"]:
                    dma(*a, **k)

            def win(i):
                s = i * SC
                e = min(s + SC, T)
                return s, e, e - s, e - s + 2

            def rms_stats(X, lo, hi, Pt, sq, sd, rs):
                n = hi - lo
                dop("act", lambda: A.activation(out=sq[:, :, 0:n], in_=X[:, :, lo:hi], func=AF.Square), reads=[X], writes=[sq])

                def mm():
                    for c in range(8):
                        last = PE.matmul(Pt[:, 0:n], lhsT=ones_bf, rhs=sq[:, c, 0:n], start=(c == 0), stop=(c == 7))
                    return last
                dop("pe", mm, reads=[sq, cbf], writes=[Pt])
                dop("act", lambda: A.activation(out=sd[:, 0:n], in_=Pt[:, 0:n], func=AF.Sqrt, scale=1.0 / D, bias=EPS),
                    reads=[Pt], writes=[sd])
                dop("dve", lambda: V.reciprocal(out=rs[:, 0:n], in_=sd[:, 0:n]), reads=[sd], writes=[rs])

            def front(i):
                s, e, Wc, W = win(i)
                X = xw[i % 2]
                attn = attn2[i % 2]
                ddma("sp", X[:, :, 0:W], xv[:, :, s + 1:e + 3], writes=[X])
                ddma("sp", tbw[:, :, 0:W], tab_d[:, :, s + 1:e + 3], writes=[tbw])
                rms_stats(X, 0, W, po, sqF, sdF, rsF)
                for c in range(8):
                    dop("pool", lambda c=c: G.tensor_scalar(out=xb[:, c, 0:W], in0=X[:, c, 0:W],
                                                             scalar1=pvec[:, PV_G1 + c:PV_G1 + c + 1], scalar2=0.0,
                                                             op0=ALU.mult, op1=ALU.add), reads=[X, pvec], writes=[xb])
                yield 6.0
                for j in range(4):
                    wt = get_item("wq", j)
                    pq = pden

                    def q_mm(wt=wt, pq=pq):
                        for kc in range(8):
                            last = PE.matmul(pq[:, 0:W], lhsT=wt[:, kc * 128:(kc + 1) * 128], rhs=xb[:, kc, 0:W],
                                             start=(kc == 0), stop=(kc == 7))
                        return last
                    dop("pe", q_mm, reads=[wt, xb], writes=[pq])
                    dop("dve", lambda pq=pq: V.tensor_tensor(out=qz[:, 0:W], in0=pq[:, 0:W], in1=rsF[:, 0:W], op=ALU.mult),
                        reads=[pq, rsF], writes=[qz])
                    dop("act", lambda: A.activation(out=qsq[:, 0:W], in_=qz[:, 0:W], func=AF.Square), reads=[qz], writes=[qsq])
                    dop("pe", lambda: PE.matmul(po[:, 0:W], lhsT=bd_bf, rhs=qsq[:, 0:W], start=True, stop=True),
                        reads=[cbf, qsq], writes=[po])
                    dop("act", lambda: A.activation(out=qsd[:, 0:W], in_=po[:, 0:W], func=AF.Sqrt, scale=1.0 / 64, bias=EPS),
                        reads=[po], writes=[qsd])
                    dop("dve", lambda: V.reciprocal(out=qsd[:, 0:W], in_=qsd[:, 0:W]), reads=[qsd], writes=[qsd])
                    dop("dve", lambda: V.scalar_tensor_tensor(out=qn[:, 0:W], in0=qz[:, 0:W], scalar=pvec[:, PV_GQ:PV_GQ + 1],
                                                              in1=qsd[:, 0:W], op0=ALU.mult, op1=ALU.mult),
                        reads=[qz, qsd, pvec], writes=[qn])
                    dop("pe", lambda: PE.matmul(po[:, 0:W], lhsT=R32, rhs=qn[:, 0:W], start=True, stop=True),
                        reads=[cmat, qn], writes=[po])
                    dop("pool", lambda: G.tensor_tensor(out=qt1[:, 0:W], in0=qn[:, 0:W], in1=tbw[:, 0, 0:W], op=ALU.mult),
                        reads=[qn, tbw], writes=[qt1])
                    dop("dve", lambda: V.tensor_tensor(out=qt2[:, 0:W], in0=po[:, 0:W], in1=tbw[:, 1, 0:W], op=ALU.mult),
                        reads=[po, tbw], writes=[qt2])
                    dop("dve", lambda j=j: V.tensor_tensor(out=qT[:, j, 0:W], in0=qt1[:, 0:W], in1=qt2[:, 0:W], op=ALU.add),
                        reads=[qt1, qt2], writes=[qT])
                    yield 8.0
                for j in range(4):
                    def sc_mm(kt, j=j):
                        Sx = SF[kt % 2]
                        PE.matmul(Sx[:, 0, 0:W], lhsT=kT[0:64, kt * 128:(kt + 1) * 128], rhs=qT[0:64, j, 0:W], start=True, stop=True)
                        return PE.matmul(Sx[:, 1, 0:W], lhsT=kT[64:128, kt * 128:(kt + 1) * 128], rhs=qT[64:128, j, 0:W],
                                         start=True, stop=True)
                    dop("pe", lambda: sc_mm(0), reads=[kT, qT], writes=[SF[0]])
                    for kt in range(32):
                        if kt + 1 < 32:
                            dop("pe", lambda kt=kt: sc_mm(kt + 1), reads=[kT, qT], writes=[SF[(kt + 1) % 2]])
                        Sx = SF[kt % 2]
                        E = ET[kt % 3]
                        dop("act", lambda Sx=Sx, E=E: A.activation(out=E[:, :, 0:W], in_=Sx[:, :, 0:W], func=AF.Exp, scale=0.125),
                            reads=[Sx], writes=[E])

                        def pv_mm(kt=kt, E=E):
                            f = (kt == 0)
                            l = (kt == 31)
                            PE.matmul(po[0:64, 0:W], lhsT=vt[:, kt, 0:64], rhs=E[:, 0, 0:W], start=f, stop=l)
                            PE.matmul(po[64:128, 0:W], lhsT=vt[:, kt, 64:128], rhs=E[:, 1, 0:W], start=f, stop=l)
                            PE.matmul(pden[0:64, 0:W], lhsT=ones_bf[:, 0:64], rhs=E[:, 0, 0:W], start=f, stop=l)
                            return PE.matmul(pden[64:128, 0:W], lhsT=ones_bf[:, 64:128], rhs=E[:, 1, 0:W], start=f, stop=l)
                        dop("pe", pv_mm, reads=[vt, E, cbf], writes=[po, pden])
                        yield 1.0
                    dop("dve", lambda: V.reciprocal(out=rden[:, 0:W], in_=pden[:, 0:W]), reads=[pden], writes=[rden])
                    dop("dve", lambda j=j: V.tensor_tensor(out=attn[:, j, 0:W], in0=po[:, 0:W], in1=rden[:, 0:W], op=ALU.mult),
                        reads=[po, rden], writes=[attn])
                    yield 1.0

            def back(i):
                s, e, Wc, W = win(i)
                X = xw[i % 2]
                attn = attn2[i % 2]
                for m in range(8):
                    wt = get_item("wo", m)
                    pm = PB[m % 2]

                    def o_mm(wt=wt, pm=pm):
                        for rc in range(8):
                            rhs = attn[:, rc, 0:W] if rc < 4 else lru[:, rc - 4, s:s + W]
                            last = PE.matmul(pm[:, 0:W], lhsT=wt[:, rc * 128:(rc + 1) * 128], rhs=rhs, start=(rc == 0), stop=(rc == 7))
                        return last
                    dop("pe", o_mm, reads=[wt, attn, lru], writes=[pm])
                    dop("dve", lambda m=m, pm=pm: V.tensor_tensor(out=X[:, m, 0:W], in0=pm[:, 0:W], in1=X[:, m, 0:W], op=ALU.add),
                        reads=[pm, X], writes=[X])
                    yield 1.5
                rms_stats(X, 0, W, PB[0], sqB, sdB, rsB)
                for m in range(8):
                    dop("dve", lambda m=m: V.scalar_tensor_tensor(out=h2[:, m, 0:W], in0=X[:, m, 0:W],
                                                                  scalar=pvec[:, PV_G2 + m:PV_G2 + m + 1], in1=rsB[:, 0:W],
                                                                  op0=ALU.mult, op1=ALU.mult), reads=[X, rsB, pvec], writes=[h2])
                if i == 0:
                    dop("pool", lambda: G.memset(h2[:, :, 0:1], 0.0), writes=[h2])
                if e == T:
                    dop("pool", lambda: G.memset(h2[:, :, W - 1:W], 0.0), writes=[h2])
                yield 8.0
                for jj in range(24):
                    wt = get_item("wu", jj)
                    Bg, Bv = PB[0], PB[1]

                    def u_mm(wt=wt):
                        for kc in range(8):
                            PE.matmul(Bg[:, 0:W], lhsT=wt[:, kc * 128:(kc + 1) * 128], rhs=h2[:, kc, 0:W],
                                      start=(kc == 0), stop=(kc == 7))
                        for kc in range(8):
                            last = PE.matmul(Bv[:, 0:W], lhsT=wt[:, 1024 + kc * 128:1024 + (kc + 1) * 128], rhs=h2[:, kc, 0:W],
                                             start=(kc == 0), stop=(kc == 7))
                        return last
                    dop("pe", u_mm, reads=[wt, h2], writes=[Bg, Bv])
                    tg, tv, fs = TG[jj % 2], TV[jj % 2], FS[jj % 2]
                    for (Bx, dstt, ch) in ((Bg, tg, jj), (Bv, tv, 24 + jj)):
                        w0c = PV_UCW + 0 * 48 + ch
                        w1c = PV_UCW + 1 * 48 + ch
                        w2c = PV_UCW + 2 * 48 + ch
                        bc = PV_UCB + ch
                        dop("act", lambda Bx=Bx, dstt=dstt, w1c=w1c, bc=bc: A.activation(
                            out=dstt[:, 0:Wc], in_=Bx[:, 1:1 + Wc], func=AF.Identity, scale=pvec[:, w1c:w1c + 1],
                            bias=pvec[:, bc:bc + 1]), reads=[Bx, pvec], writes=[dstt])
                        dop("dve", lambda Bx=Bx, dstt=dstt, w0c=w0c: V.scalar_tensor_tensor(
                            out=dstt[:, 0:Wc], in0=Bx[:, 0:Wc], scalar=pvec[:, w0c:w0c + 1], in1=dstt[:, 0:Wc],
                            op0=ALU.mult, op1=ALU.add), reads=[Bx, pvec, dstt], writes=[dstt])
                        dop("dve", lambda Bx=Bx, dstt=dstt, w2c=w2c: V.scalar_tensor_tensor(
                            out=dstt[:, 0:Wc], in0=Bx[:, 2:2 + Wc], scalar=pvec[:, w2c:w2c + 1], in1=dstt[:, 0:Wc],
                            op0=ALU.mult, op1=ALU.add), reads=[Bx, pvec, dstt], writes=[dstt])
                    dop("act", lambda tg=tg, fs=fs: A.activation(out=fs[:, 0:Wc], in_=tg[:, 0:Wc], func=AF.Square),
                        reads=[tg], writes=[fs])
                    dop("pool", lambda fs=fs: G.tensor_scalar(out=fs[:, 0:Wc], in0=fs[:, 0:Wc], scalar1=0.044715, scalar2=1.0,
                                                               op0=ALU.mult, op1=ALU.add), reads=[fs], writes=[fs])
                    dop("pool", lambda fs=fs, tg=tg: G.tensor_tensor(out=fs[:, 0:Wc], in0=fs[:, 0:Wc], in1=tg[:, 0:Wc], op=ALU.mult),
                        reads=[fs, tg], writes=[fs])
                    dop("act", lambda fs=fs: A.activation(out=fs[:, 0:Wc], in_=fs[:, 0:Wc], func=AF.Tanh, scale=GC),
                        reads=[fs], writes=[fs])
                    dop("dve", lambda fs=fs, tg=tg: V.scalar_tensor_tensor(out=tg[:, 0:Wc], in0=fs[:, 0:Wc], scalar=1.0, in1=tg[:, 0:Wc],
                                                                          op0=ALU.add, op1=ALU.mult), reads=[fs, tg], writes=[tg])
                    dop("dve", lambda jj=jj, tg=tg, tv=tv: V.scalar_tensor_tensor(out=aall[:, jj, 0:Wc], in0=tg[:, 0:Wc], scalar=0.5,
                                                                                  in1=tv[:, 0:Wc], op0=ALU.mult, op1=ALU.mult),
                        reads=[tg, tv], writes=[aall])
                    yield 5.5
                for m in range(8):
                    wt = get_item("wd", m)
                    pm = PB[m % 2]

                    def d_mm(wt=wt, pm=pm):
                        for jc in range(24):
                            last = PE.matmul(pm[:, 0:Wc], lhsT=wt[:, jc * 128:(jc + 1) * 128], rhs=aall[:, jc, 0:Wc],
                                             start=(jc == 0), stop=(jc == 23))
                        return last
                    dop("pe", d_mm, reads=[wt, aall], writes=[pm])
                    dop("dve", lambda m=m, pm=pm: V.tensor_tensor(out=X[:, m, 1:1 + Wc], in0=pm[:, 0:Wc], in1=X[:, m, 1:1 + Wc],
                                                                  op=ALU.add), reads=[pm, X], writes=[X])
                    yield 5.0
                rms_stats(X, 1, 1 + Wc, PB[0], sqB, sdB, rsB)
                for m in range(8):
                    dop("dve", lambda m=m: V.scalar_tensor_tensor(out=X[:, m, 1:1 + Wc], in0=X[:, m, 1:1 + Wc],
                                                                  scalar=pvec[:, PV_GF + m:PV_GF + m + 1], in1=rsB[:, 0:Wc],
                                                                  op0=ALU.mult, op1=ALU.mult), reads=[X, rsB, pvec], writes=[X])
                ddma("sp", ov[:, :, s:e], X[:, :, 1:1 + Wc], reads=[X], semtile=X)
                yield 8.0

            def schedule():
                def run_pair(ga, gb):
                    ta = tb_ = 0.0
                    da = ga is None
                    db = gb is None
                    while not (da and db):
                        if not da and (db or ta <= tb_):
                            try:
                                ta += next(ga)
                            except StopIteration:
                                da = True
                        else:
                            try:
                                tb_ += next(gb)
                            except StopIteration:
                                db = True
                run_pair(front(0), None)
                for i in range(nwc):
                    run_pair(front(i + 1) if i + 1 < nwc else None, back(i))

            if nwc > 0:
                st["dry"] = True
                schedule()
                st["dry"] = False
                issue_upto(NSLOT)
                schedule()
            C.barrier(dma_tiles=xw)
    return nc, dbg


def _rope_tables():
    half = 32
    inv_freq = (1.0 / (np.float32(10000.0) ** (np.arange(0, half, 2, dtype=np.float32) / np.float32(half)))).astype(np.float32)
    t = np.arange(T)
    row = (t // 64).astype(np.float32)
    col = (t % 64).astype(np.float32)
    tab = np.zeros((128, 2, T + 4), np.float32)
    for p in range(128):
        d = p % 64
        pos = row if d < 32 else col
        w = d % 32
        fi = w % 16
        ang = (pos * inv_freq[fi]).astype(np.float32)
        tab[p, 0, 2:T + 2] = np.cos(ang)
        sn = np.sin(ang)
        tab[p, 1, 2:T + 2] = -sn if w < 16 else sn
    return tab


def _consts():
    cm = np.zeros((128, 512), np.float32)
    cm[:, 0:128] = 1.0
    cm[0:64, 128:192] = 1.0
    cm[64:128, 192:256] = 1.0
    for m in range(128):
        w = (m % 64) % 32
        partner = m + 16 if w < 16 else m - 16
        cm[partner, 256 + m] = 1.0
    cm[:, 384:512] = np.eye(128, dtype=np.float32)
    return cm


def _host_prep(inp):
    f = lambda a: np.ascontiguousarray(np.asarray(a, dtype=np.float32))
    perm = []
    for j in range(4):
        perm += list(range(j * 64, (j + 1) * 64)) + list(range((4 + j) * 64, (5 + j) * 64))
    perm = np.array(perm)
    w_in = f(inp["w_in"])[0]
    wq = np.ascontiguousarray(w_in[:, 0:512][:, perm])
    wkvl = np.ascontiguousarray(w_in[:, 512:1792])
    w_out = f(inp["w_out"])[0]
    wo = np.ascontiguousarray(np.concatenate([w_out[0:512][perm], w_out[512:]], axis=0))
    wu = f(inp["w_up"])[0]
    wd = f(inp["w_down"])[0]
    pv = np.zeros((128, PV_N), np.float32)

    def chunked(v, n):
        return np.asarray(v, np.float32).reshape(n, 128).T
    pv[:, PV_G1:PV_G1 + 8] = chunked(inp["norm1_g"][0], 8)
    pv[:, PV_G2:PV_G2 + 8] = chunked(inp["norm2_g"][0], 8)
    pv[:, PV_GF:PV_GF + 8] = chunked(inp["final_g"], 8)
    pv[:, PV_GQ] = np.tile(np.asarray(inp["q_norm_g"][0], np.float32), 2)
    pv[:, PV_GK] = np.tile(np.asarray(inp["k_norm_g"][0], np.float32), 2)
    lcw = np.asarray(inp["lru_conv_w"][0], np.float32)
    for c in range(4):
        for j in range(4):
            pv[:, PV_LCW + c * 4 + j] = lcw[j, c * 128:(c + 1) * 128]
    pv[:, PV_LCB:PV_LCB + 4] = chunked(inp["lru_conv_b"][0], 4)
    pv[:, PV_BAF:PV_BAF + 4] = chunked(np.asarray(inp["ba_f"][0]).reshape(-1), 4)
    pv[:, PV_BXF:PV_BXF + 4] = chunked(np.asarray(inp["bx_f"][0]).reshape(-1), 4)
    pv[:, PV_BAB:PV_BAB + 4] = chunked(np.asarray(inp["ba_b"][0]).reshape(-1), 4)
    pv[:, PV_BXB:PV_BXB + 4] = chunked(np.asarray(inp["bx_b"][0]).reshape(-1), 4)
    pv[:, PV_LAMF:PV_LAMF + 4] = chunked(inp["lam_f"][0], 4)
    pv[:, PV_LAMB:PV_LAMB + 4] = chunked(inp["lam_b"][0], 4)
    ucw = np.asarray(inp["up_conv_w"][0], np.float32)
    for tap in range(3):
        pv[:, PV_UCW + tap * 48:PV_UCW + (tap + 1) * 48] = chunked(ucw[tap], 48)
    pv[:, PV_UCB:PV_UCB + 48] = chunked(inp["up_conv_b"][0], 48)
    gate = np.zeros((128, 16 * 128), np.float32)
    for kind, name in enumerate(("wa_f", "wx_f", "wa_b", "wx_b")):
        w = np.asarray(inp[name][0], np.float32)
        for c in range(4):
            base = (kind * 4 + c) * 128
            gate[0:64, base:base + 64] = w[2 * c]
            gate[64:128, base + 64:base + 128] = w[2 * c + 1]
    shared = {"wq": wq, "wkvl": wkvl, "wo": wo, "wu": wu, "wd": wd, "pvec": pv, "gatew": gate,
              "cmat": _consts(), "tab": _rope_tables()}
    x = np.asarray(inp["x"], np.float32)
    maps = []
    for b in range(x.shape[0]):
        xT = np.zeros((D, T + 4), np.float32)
        xT[:, 2:T + 2] = x[b].T
        m = dict(shared)
        m["xT"] = xT
        maps.append(m)
    return maps


_CACHE = {}


def kernel(**inputs):
    maps = _host_prep(inputs)
    if "nc" not in _CACHE:
        _CACHE["nc"] = _build()[0]
    nc = _CACHE["nc"]
    res = run_bass_kernel_spmd(nc, maps, core_ids=list(range(len(maps))))
    out = np.stack([np.ascontiguousarray(r["outT"].T) for r in res.results], axis=0)
    return out.astype(np.float32)
```

```python
import numpy as np
from contextlib import ExitStack
import concourse.bass as bass
import concourse.mybir as mybir
from concourse.bass_utils import run_bass_kernel_spmd

F32 = mybir.dt.float32
BF16 = mybir.dt.bfloat16
AF = mybir.ActivationFunctionType
ALU = mybir.AluOpType

T = 4096
D = 1024
EPS = 1e-6
NWA = 8
XWA = 516
SC = 456
NWC = 9
WCMAX = SC + 2
GC = 0.7978845608028654
PV_G1, PV_G2, PV_GF, PV_GQ, PV_GK = 0, 8, 16, 24, 25
PV_LCW, PV_LCB = 26, 42
PV_BAF, PV_BXF, PV_BAB, PV_BXB = 46, 50, 54, 58
PV_LAMF, PV_LAMB = 62, 66
PV_UCW, PV_UCB = 70, 214
PV_N = 262
DV_HBAF, DV_HBXF, DV_HBAB, DV_HBXB = 0, 4, 8, 12
DV_CLF, DV_CLB, DV_HCLF, DV_HCLB = 16, 20, 24, 28
DV_N = 32
ITEMS = [("wq", j, 1024) for j in range(4)] + [("wo", m, 1024) for m in range(8)] + \
        [("wu", j, 2048) for j in range(24)] + [("wd", m, 3072) for m in range(8)]
ITEM_OFF = []
_o = 0
for _it in ITEMS:
    ITEM_OFF.append(_o)
    _o += _it[2]
SCR_W = _o
NSLOT = 4
SLOT_W = 3072


class Tl:
    __slots__ = ("t", "w", "r", "sem", "cnt", "name")

    def __init__(self, t, name):
        self.t = t
        self.w = None
        self.r = []
        self.sem = None
        self.cnt = 0
        self.name = name

    def __getitem__(self, k):
        return self.t[k]


class Ctx:
    def __init__(self, nc, es):
        self.nc = nc
        self.es = es
        self.eng = {"pe": nc.tensor, "act": nc.scalar, "dve": nc.vector, "pool": nc.gpsimd, "sp": nc.sync}
        self.sem = {k: es.enter_context(nc.semaphore("s_" + k)) for k in ("pe", "act", "dve", "pool")}
        self.cnt = {k: 0 for k in ("pe", "act", "dve", "pool")}
        self.seen = {k: {} for k in self.eng}
        self.nsem = 0

    def tile(self, es, name, shape, dt):
        return Tl(es.enter_context(self.nc.sbuf_tensor("t_" + name, shape, dt)), name)

    def ptile(self, es, name, shape, dt=F32):
        return Tl(es.enter_context(self.nc.psum_tensor("t_" + name, shape, dt)), name)

    def dram(self, name):
        return Tl(None, name)

    def _dsem(self, tl):
        if tl.sem is None:
            tl.sem = self.es.enter_context(self.nc.semaphore("d%d_%s" % (self.nsem, tl.name)))
            self.nsem += 1
        return tl.sem

    def _wait(self, e, deps):
        seen = self.seen[e]
        need = {}
        for d in deps:
            if d is None:
                continue
            if d[0] == "dma":
                key, val = ("dma", id(d[1])), d[2]
                semh = d[1]
            else:
                if d[0] == e and e in ("pe", "sp"):
                    continue
                key, val = d[0], d[1]
                semh = self.sem[d[0]]
            if seen.get(key, 0) >= val:
                continue
            if key not in need or need[key][1] < val:
                need[key] = (semh, val)
        for key, (semh, val) in need.items():
            self.eng[e].wait_ge(semh, val)
            seen[key] = val

    def _deps(self, reads, writes):
        deps = []
        for t in reads:
            deps.append(t.w)
        for t in writes:
            deps.append(t.w)
            deps.extend(t.r)
        return deps

    def op(self, e, fn, reads=(), writes=()):
        self._wait(e, self._deps(reads, writes))
        ins = fn()
        self.cnt[e] += 1
        ins.then_inc(self.sem[e], 1)
        tag = (e, self.cnt[e])
        for t in writes:
            t.w = tag
            t.r = []
        for t in reads:
            if t.w is not None and t.w == tag:
                continue
            t.r.append(tag)
        return ins

    def dma(self, q, out_ap, in_ap, reads=(), writes=(), semtile=None):
        self._wait(q, self._deps(reads, writes))
        st = semtile if semtile is not None else (writes[0] if writes else reads[0])
        semh = self._dsem(st)
        st.cnt += 16
        self.eng[q].dma_start(out=out_ap, in_=in_ap).then_inc(semh, 16)
        tag = ("dma", semh, st.cnt)
        for t in writes:
            t.w = tag
            t.r = []
        for t in reads:
            t.r.append(tag)

    def barrier(self, dma_tiles=()):
        for e in ("pe", "act", "dve", "pool", "sp"):
            deps = [(o, self.cnt[o]) for o in ("pe", "act", "dve", "pool") if self.cnt[o] > 0]
            for t in dma_tiles:
                if t.sem is not None:
                    deps.append(("dma", t.sem, t.cnt))
            self._wait(e, deps)


def _build(debug=False, nwc=NWC):
    nc = bass.Bass("TRN2", target_bir_lowering=False)
    dt = nc.dram_tensor
    x_d = dt("xT", [D, T + 4], F32, kind="ExternalInput").ap()
    wq_d = dt("wq", [D, 512], F32, kind="ExternalInput").ap()
    wkvl_d = dt("wkvl", [D, 1280], F32, kind="ExternalInput").ap()
    wo_d = dt("wo", [D, D], F32, kind="ExternalInput").ap()
    wu_d = dt("wu", [D, 6144], F32, kind="ExternalInput").ap()
    wd_d = dt("wd", [3072, D], F32, kind="ExternalInput").ap()
    pvec_d = dt("pvec", [128, PV_N], F32, kind="ExternalInput").ap()
    gatew_d = dt("gatew", [128, 16 * 128], F32, kind="ExternalInput").ap()
    cmat_d = dt("cmat", [128, 4 * 128], F32, kind="ExternalInput").ap()
    tab_d = dt("tab", [128, 2, T + 4], F32, kind="ExternalInput").ap()
    out_d = dt("outT", [D, T], F32, kind="ExternalOutput").ap()
    scr_d = dt("wscr", [128, SCR_W], BF16).ap()
    dbg = {}
    if debug:
        dbg["k"] = dt("dbg_k", [128, T], BF16, kind="ExternalOutput").ap()
        dbg["v"] = dt("dbg_v", [128, 32 * 128], BF16, kind="ExternalOutput").ap()
        dbg["lru"] = dt("dbg_lru", [128, 4 * (T + 2)], BF16, kind="ExternalOutput").ap()

    xv = x_d.rearrange("(c p) t -> p c t", p=128)
    ov = out_d.rearrange("(c p) t -> p c t", p=128)

    with ExitStack() as es:
        C = Ctx(nc, es)
        op, dma = C.op, C.dma
        V = nc.vector
        A = nc.scalar
        G = nc.gpsimd
        PE = nc.tensor

        pvec = C.tile(es, "pvec", [128, PV_N], F32)
        dv = C.tile(es, "dv", [128, DV_N], F32)
        cmat = C.tile(es, "cmat", [128, 512], F32)
        cbf = C.tile(es, "cbf", [128, 512], BF16)
        gw = C.tile(es, "gw", [128, 16 * 128], BF16)
        kT = C.tile(es, "kT", [128, T], BF16)
        vt = C.tile(es, "vt", [128, 32, 128], BF16)
        lru = C.tile(es, "lru", [128, 4, T + 2], BF16)
        carry = C.tile(es, "carry", [128, 4], F32)
        S = [C.ptile(es, "S%d" % i, [128, 2, 512]) for i in range(2)]
        P = [C.ptile(es, "P%d" % i, [128, 512]) for i in range(4)]
        scr = C.dram("scr")
        ones_bf = cbf[:, 0:128]
        bd_bf = cbf[:, 128:256]
        ident_bf = cbf[:, 384:512]
        R32 = cmat[:, 256:384]

        dma("sp", pvec[:, :], pvec_d[:, :], writes=[pvec])
        dma("sp", cmat[:, :], cmat_d[:, :], writes=[cmat])
        op("pool", lambda: G.tensor_copy(out=cbf[:, :], in_=cmat[:, :]), reads=[cmat], writes=[cbf])
        op("pool", lambda: G.memset(lru[:, :, :], 0.0), writes=[lru])
        op("pool", lambda: G.memset(carry[:, :], 0.0), writes=[carry])
        for (src, dst) in ((PV_BAF, DV_HBAF), (PV_BXF, DV_HBXF), (PV_BAB, DV_HBAB), (PV_BXB, DV_HBXB)):
            op("dve", lambda src=src, dst=dst: V.tensor_scalar(
                out=dv[:, dst:dst + 4], in0=pvec[:, src:src + 4], scalar1=0.5, scalar2=None, op0=ALU.mult),
               reads=[pvec], writes=[dv])
        with ExitStack() as es0:
            et0 = C.tile(es0, "s_e", [128, 8], F32)
            e2 = C.tile(es0, "s_e2", [128, 8], F32)
            acc = C.tile(es0, "s_acc", [128, 8], F32)
            op("act", lambda: A.activation(out=et0[:, :], in_=pvec[:, PV_LAMF:PV_LAMF + 8], func=AF.Exp, scale=-1.0),
               reads=[pvec], writes=[et0])
            op("dve", lambda: V.tensor_scalar(out=acc[:, :], in0=et0[:, :], scalar1=-0.25, scalar2=1.0 / 3.0,
                                              op0=ALU.mult, op1=ALU.add), reads=[et0], writes=[acc])
            op("dve", lambda: V.tensor_tensor(out=acc[:, :], in0=acc[:, :], in1=et0[:, :], op=ALU.mult),
               reads=[et0, acc], writes=[acc])
            op("dve", lambda: V.tensor_scalar(out=acc[:, :], in0=acc[:, :], scalar1=-0.5, scalar2=None, op0=ALU.add),
               reads=[acc], writes=[acc])
            op("dve", lambda: V.tensor_tensor(out=acc[:, :], in0=acc[:, :], in1=et0[:, :], op=ALU.mult),
               reads=[et0, acc], writes=[acc])
            op("dve", lambda: V.tensor_scalar(out=acc[:, :], in0=acc[:, :], scalar1=1.0, scalar2=None, op0=ALU.add),
               reads=[acc], writes=[acc])
            op("dve", lambda: V.tensor_tensor(out=e2[:, :], in0=acc[:, :], in1=et0[:, :], op=ALU.mult),
               reads=[et0, acc], writes=[e2])
            op("dve", lambda: V.tensor_scalar(out=dv[:, DV_CLF:DV_CLF + 8], in0=e2[:, :], scalar1=-8.0, scalar2=None,
                                              op0=ALU.mult), reads=[e2], writes=[dv])
            op("dve", lambda: V.tensor_scalar(out=dv[:, DV_HCLF:DV_HCLF + 8], in0=e2[:, :], scalar1=-4.0, scalar2=None,
                                              op0=ALU.mult), reads=[e2], writes=[dv])
            C.barrier()

        with ExitStack() as esA:
            wk = C.tile(esA, "wkvl", [128, 8, 1280], BF16)
            dma("pool", gw[:, :], gatew_d[:, :], writes=[gw])
            wkv = wkvl_d.rearrange("(c p) n -> p c n", p=128)
            for c in range(8):
                dma("pool", wk[:, c, :], wkv[:, c, :], writes=[wk])
            wqv = wq_d.rearrange("(c p) n -> p c n", p=128)
            wov = wo_d.rearrange("(c p) n -> p c n", p=128)
            wuv = wu_d.rearrange("(c p) n -> p c n", p=128)
            wdv = wd_d.rearrange("(c p) n -> p c n", p=128)
            scr_sem = C._dsem(scr)

            def cast(dst, src):
                nc.gpsimd.dma_start(out=dst, in_=src).then_inc(scr_sem, 16)
                scr.cnt += 16
            for it, off in zip(ITEMS, ITEM_OFF):
                kind, j, w = it
                if kind == "wq":
                    cast(scr_d[:, off:off + w].rearrange("p (c n) -> p c n", c=8), wqv[:, :, j * 128:(j + 1) * 128])
                elif kind == "wo":
                    cast(scr_d[:, off:off + w].rearrange("p (c n) -> p c n", c=8), wov[:, :, j * 128:(j + 1) * 128])
                elif kind == "wu":
                    cast(scr_d[:, off:off + 1024].rearrange("p (c n) -> p c n", c=8), wuv[:, :, j * 128:(j + 1) * 128])
                    cast(scr_d[:, off + 1024:off + 2048].rearrange("p (c n) -> p c n", c=8),
                         wuv[:, :, 3072 + j * 128:3072 + (j + 1) * 128])
                else:
                    cast(scr_d[:, off:off + w].rearrange("p (c n) -> p c n", c=24), wdv[:, :, j * 128:(j + 1) * 128])
            scr.w = ("dma", scr_sem, scr.cnt)

            xw = [C.tile(esA, "xwA%d" % i, [128, 8, XWA], F32) for i in range(2)]
            sq = C.tile(esA, "sqA", [128, 8, XWA], BF16)
            xb = C.tile(esA, "xbA", [128, 8, XWA], BF16)
            sd = C.tile(esA, "sdA", [128, XWA], F32)
            rs = C.tile(esA, "rsA", [128, XWA], F32)
            xr = C.tile(esA, "xrA", [128, 4, XWA], F32)
            tb = C.tile(esA, "tabA", [128, 2, 512], F32)
            gen = [C.tile(esA, "gen%d" % i, [128, 512], F32) for i in range(5)]
            ksq = C.tile(esA, "ksq", [128, 512], BF16)
            vzb = C.tile(esA, "vzb", [128, 512], BF16)
            XC = [C.tile(esA, "xc%d" % i, [128, 512], F32) for i in range(4)]
            UU = [C.tile(esA, "uu%d" % i, [128, 512], F32) for i in range(4)]
            A2 = [C.tile(esA, "a2%d" % i, [128, 512], F32) for i in range(4)]
            AA = [C.tile(esA, "aa%d" % i, [128, 512], F32) for i in range(4)]
            TR = [C.tile(esA, "tr%d" % i, [128, 512], F32) for i in range(2)]
            HH = [C.tile(esA, "hh%d" % i, [128, 512], F32) for i in range(2)]
            XCB = [C.tile(esA, "xcb%d" % i, [128, 512], BF16) for i in range(2)]

            def load_xA(i, slot):
                s = i * 512
                dma("sp", xw[slot][:, :, 0:515], xv[:, :, s:s + 515], writes=[xw[slot]])

            def passA(fwd):
                order = list(range(NWA)) if fwd else list(range(NWA - 1, -1, -1))
                load_xA(order[0], 0)
                kw = 0 if fwd else 2
                hba = DV_HBAF if fwd else DV_HBAB
                hbx = DV_HBXF if fwd else DV_HBXB
                cl = DV_CLF if fwd else DV_CLB
                hcl = DV_HCLF if fwd else DV_HCLB
                for n, i in enumerate(order):
                    slot = n % 2
                    s = i * 512
                    X = xw[slot]
                    if n + 1 < len(order):
                        load_xA(order[n + 1], (n + 1) % 2)
                    if fwd:
                        dma("sp", tb[:, :, :], tab_d[:, :, s + 2:s + 514], writes=[tb])
                    op("act", lambda: A.activation(out=sq[:, :, 0:515], in_=X[:, :, 0:515], func=AF.Square),
                       reads=[X], writes=[sq])

                    def ss_mm():
                        for c in range(8):
                            PE.matmul(P[0][:, 0:512], lhsT=ones_bf, rhs=sq[:, c, 0:512], start=(c == 0), stop=(c == 7))
                        for c in range(8):
                            last = PE.matmul(P[1][:, 0:4], lhsT=ones_bf, rhs=sq[:, c, 511:515], start=(c == 0), stop=(c == 7))
                        return last
                    op("pe", ss_mm, reads=[sq, cbf], writes=[P[0], P[1]])
                    op("act", lambda: A.activation(out=sd[:, 0:512], in_=P[0][:, 0:512], func=AF.Sqrt, scale=1.0 / D, bias=EPS),
                       reads=[P[0]], writes=[sd])
                    op("act", lambda: A.activation(out=sd[:, 512:515], in_=P[1][:, 1:4], func=AF.Sqrt, scale=1.0 / D, bias=EPS),
                       reads=[P[1], sd], writes=[sd])
                    op("dve", lambda: V.reciprocal(out=rs[:, 0:515], in_=sd[:, 0:515]), reads=[sd], writes=[rs])
                    for c in range(8):
                        op("pool", lambda c=c: G.tensor_scalar(out=xb[:, c, 0:515], in0=X[:, c, 0:515],
                                                                scalar1=pvec[:, PV_G1 + c:PV_G1 + c + 1], scalar2=0.0,
                                                                op0=ALU.mult, op1=ALU.add),
                           reads=[X, pvec], writes=[xb])
                    for c in range(4):
                        col0 = 256 + c * 128
                        pm = P[2 + (c % 2)]

                        def xr_mm(col0=col0, pm=pm):
                            for kc in range(8):
                                last = PE.matmul(pm[:, 0:512], lhsT=wk[:, kc, col0:col0 + 128], rhs=xb[:, kc, 0:512],
                                                 start=(kc == 0), stop=(kc == 7))
                            return last
                        op("pe", xr_mm, reads=[wk, xb], writes=[pm])
                        op("dve", lambda c=c, pm=pm: V.tensor_tensor(out=xr[:, c, 0:512], in0=pm[:, 0:512], in1=rs[:, 0:512],
                                                                     op=ALU.mult), reads=[pm, rs], writes=[xr])

                    def xrh_mm():
                        for c in range(4):
                            col0 = 256 + c * 128
                            for kc in range(8):
                                last = PE.matmul(P[1][:, 16 + c * 4:16 + c * 4 + 4], lhsT=wk[:, kc, col0:col0 + 128],
                                                 rhs=xb[:, kc, 511:515], start=(kc == 0), stop=(kc == 7), skip_group_check=True)
                        return last
                    op("pe", xrh_mm, reads=[wk, xb], writes=[P[1]])
                    for c in range(4):
                        op("dve", lambda c=c: V.tensor_tensor(out=xr[:, c, 512:515], in0=P[1][:, 16 + c * 4 + 1:16 + c * 4 + 4],
                                                              in1=rs[:, 512:515], op=ALU.mult), reads=[P[1], rs], writes=[xr])
                    if fwd:
                        kz, ksd, kn, kt1, kt2 = gen

                        def k_mm():
                            for kc in range(8):
                                last = PE.matmul(P[2][:, 0:512], lhsT=wk[:, kc, 0:128], rhs=xb[:, kc, 2:514],
                                                 start=(kc == 0), stop=(kc == 7))
                            return last
                        op("pe", k_mm, reads=[wk, xb], writes=[P[2]])
                        op("dve", lambda: V.tensor_tensor(out=kz[:, :], in0=P[2][:, :], in1=rs[:, 2:514], op=ALU.mult),
                           reads=[P[2], rs], writes=[kz])
                        op("act", lambda: A.activation(out=ksq[:, :], in_=kz[:, :], func=AF.Square), reads=[kz], writes=[ksq])
                        op("pe", lambda: PE.matmul(P[2][:, :], lhsT=bd_bf, rhs=ksq[:, :], start=True, stop=True),
                           reads=[cbf, ksq], writes=[P[2]])
                        op("act", lambda: A.activation(out=ksd[:, :], in_=P[2][:, :], func=AF.Sqrt, scale=1.0 / 64, bias=EPS),
                           reads=[P[2]], writes=[ksd])
                        op("dve", lambda: V.reciprocal(out=ksd[:, :], in_=ksd[:, :]), reads=[ksd], writes=[ksd])
                        op("dve", lambda: V.scalar_tensor_tensor(out=kn[:, :], in0=kz[:, :], scalar=pvec[:, PV_GK:PV_GK + 1],
                                                                 in1=ksd[:, :], op0=ALU.mult, op1=ALU.mult),
                           reads=[kz, ksd, pvec], writes=[kn])
                        op("pe", lambda: PE.matmul(P[2][:, :], lhsT=R32, rhs=kn[:, :], start=True, stop=True),
                           reads=[cmat, kn], writes=[P[2]])
                        op("pool", lambda: G.tensor_tensor(out=kt1[:, :], in0=kn[:, :], in1=tb[:, 0, :], op=ALU.mult),
                           reads=[kn, tb], writes=[kt1])
                        op("dve", lambda: V.tensor_tensor(out=kt2[:, :], in0=P[2][:, :], in1=tb[:, 1, :], op=ALU.mult),
                           reads=[P[2], tb], writes=[kt2])
                        op("dve", lambda: V.tensor_tensor(out=kT[:, s:s + 512], in0=kt1[:, :], in1=kt2[:, :], op=ALU.add),
                           reads=[kt1, kt2], writes=[kT])

                        def v_mm():
                            for kc in range(8):
                                last = PE.matmul(P[3][:, 0:512], lhsT=wk[:, kc, 128:256], rhs=xb[:, kc, 2:514],
                                                 start=(kc == 0), stop=(kc == 7))
                            return last
                        op("pe", v_mm, reads=[wk, xb], writes=[P[3]])
                        op("dve", lambda: V.tensor_tensor(out=vzb[:, :], in0=P[3][:, :], in1=rs[:, 2:514], op=ALU.mult),
                           reads=[P[3], rs], writes=[vzb])
                        pvb = P[3][:, 0:256].bitcast(BF16)

                        def v_tr():
                            for q in range(4):
                                last = PE.transpose(out=pvb[:, q * 128:(q + 1) * 128], in_=vzb[:, q * 128:(q + 1) * 128],
                                                    identity=ident_bf)
                            return last
                        op("pe", v_tr, reads=[vzb, cbf], writes=[P[3]])
                        op("act", lambda: A.activation(out=vt[:, i * 4:i * 4 + 4, :], in_=pvb.rearrange("p (q n) -> p q n", q=4),
                                                       func=AF.Copy), reads=[P[3]], writes=[vt])
                    for c in range(4):
                        w0 = PV_LCW + c * 4
                        op("act", lambda c=c, w0=w0: A.activation(out=XC[c][:, :], in_=xr[:, c, 0:512], func=AF.Identity,
                                                                  scale=pvec[:, w0:w0 + 1], bias=pvec[:, PV_LCB + c:PV_LCB + c + 1]),
                           reads=[xr, pvec], writes=[XC[c]])
                        for j in range(1, 4):
                            op("dve", lambda c=c, j=j, w0=w0: V.scalar_tensor_tensor(
                                out=XC[c][:, :], in0=xr[:, c, j:j + 512], scalar=pvec[:, w0 + j:w0 + j + 1], in1=XC[c][:, :],
                                op0=ALU.mult, op1=ALU.add), reads=[xr, pvec, XC[c]], writes=[XC[c]])
                    for c in range(4):
                        xcb = XCB[c % 2]
                        trc = TR[c % 2]
                        Sg = S[c % 2]
                        op("pool", lambda c=c, xcb=xcb: G.tensor_copy(out=xcb[:, :], in_=XC[c][:, :]), reads=[XC[c]], writes=[xcb])

                        def g_mm(c=c, xcb=xcb, Sg=Sg):
                            PE.matmul(Sg[:, 0, :], lhsT=gw[:, (kw * 4 + c) * 128:(kw * 4 + c + 1) * 128], rhs=xcb[:, :],
                                      start=True, stop=True)
                            return PE.matmul(Sg[:, 1, :], lhsT=gw[:, ((kw + 1) * 4 + c) * 128:((kw + 1) * 4 + c + 1) * 128],
                                             rhs=xcb[:, :], start=True, stop=True)
                        op("pe", g_mm, reads=[gw, xcb], writes=[Sg])
                        op("act", lambda c=c, trc=trc, Sg=Sg: A.activation(out=trc[:, :], in_=Sg[:, 0, :], func=AF.Tanh, scale=0.5,
                                                                           bias=dv[:, hba + c:hba + c + 1]), reads=[Sg, dv], writes=[trc])
                        op("act", lambda c=c, Sg=Sg: A.activation(out=UU[c][:, :], in_=Sg[:, 1, :], func=AF.Tanh, scale=0.5,
                                                                  bias=dv[:, hbx + c:hbx + c + 1]), reads=[Sg, dv], writes=[UU[c]])
                        op("act", lambda c=c, trc=trc: A.activation(out=AA[c][:, :], in_=trc[:, :], func=AF.Exp,
                                                                    scale=dv[:, hcl + c:hcl + c + 1], bias=dv[:, hcl + c:hcl + c + 1]),
                           reads=[trc, dv], writes=[AA[c]])
                        op("act", lambda c=c, trc=trc: A.activation(out=A2[c][:, :], in_=trc[:, :], func=AF.Exp,
                                                                    scale=dv[:, cl + c:cl + c + 1], bias=dv[:, cl + c:cl + c + 1]),
                           reads=[trc, dv], writes=[A2[c]])
                        op("dve", lambda c=c: V.scalar_tensor_tensor(out=UU[c][:, :], in0=UU[c][:, :], scalar=1.0, in1=XC[c][:, :],
                                                                     op0=ALU.add, op1=ALU.mult), reads=[UU[c], XC[c]], writes=[UU[c]])
                    if not fwd:
                        Y1 = [gen[0], gen[1]]
                        Y2 = [gen[2], gen[3]]
                    for c in range(4):
                        op("act", lambda c=c: A.activation(out=A2[c][:, :], in_=A2[c][:, :], func=AF.Sqrt, scale=-1.0, bias=1.0),
                           reads=[A2[c]], writes=[A2[c]])
                    for c in range(4):
                        hh = HH[c % 2]
                        first = (fwd and i == 0) or ((not fwd) and i == NWA - 1)
                        if first:
                            col = 0 if fwd else 511
                            op("pool", lambda c=c, col=col: G.memset(A2[c][:, col:col + 1], 1.0), writes=[A2[c]])
                        op("dve", lambda c=c: V.scalar_tensor_tensor(out=UU[c][:, :], in0=UU[c][:, :], scalar=0.5, in1=A2[c][:, :],
                                                                     op0=ALU.mult, op1=ALU.mult), reads=[UU[c], A2[c]], writes=[UU[c]])
                        if fwd:
                            op("dve", lambda c=c, hh=hh: V.tensor_tensor_scan(out=hh[:, :], data0=AA[c][:, :], data1=UU[c][:, :],
                                                                              initial=carry[:, c:c + 1], op0=ALU.mult, op1=ALU.add),
                               reads=[AA[c], UU[c], carry], writes=[hh])
                            op("pool", lambda c=c, hh=hh: G.tensor_copy(out=carry[:, c:c + 1], in_=hh[:, 511:512]),
                               reads=[hh], writes=[carry])
                            op("pool", lambda c=c, hh=hh: G.tensor_copy(out=lru[:, c, 1 + s:1 + s + 512], in_=hh[:, :]),
                               reads=[hh], writes=[lru])
                        else:
                            op("dve", lambda c=c, hh=hh: V.tensor_tensor_scan(out=hh[:, ::-1], data0=AA[c][:, ::-1],
                                                                              data1=UU[c][:, ::-1], initial=carry[:, c:c + 1],
                                                                              op0=ALU.mult, op1=ALU.add),
                               reads=[AA[c], UU[c], carry], writes=[hh])
                            op("pool", lambda c=c, hh=hh: G.tensor_copy(out=carry[:, c:c + 1], in_=hh[:, 0:1]),
                               reads=[hh], writes=[carry])
                            y1, y2 = Y1[c % 2], Y2[c % 2]
                            pm = P[2 + (c % 2)]
                            col0 = 768 + c * 128

                            def y_mm(col0=col0, pm=pm):
                                for kc in range(8):
                                    last = PE.matmul(pm[:, 0:512], lhsT=wk[:, kc, col0:col0 + 128], rhs=xb[:, kc, 2:514],
                                                     start=(kc == 0), stop=(kc == 7))
                                return last
                            op("pe", y_mm, reads=[wk, xb], writes=[pm])
                            op("dve", lambda y1=y1, pm=pm: V.tensor_tensor(out=y1[:, :], in0=pm[:, :], in1=rs[:, 2:514], op=ALU.mult),
                               reads=[pm, rs], writes=[y1])
                            op("pool", lambda y1=y1, y2=y2: G.tensor_tensor(out=y2[:, :], in0=y1[:, :], in1=y1[:, :], op=ALU.mult),
                               reads=[y1], writes=[y2])
                            op("pool", lambda y2=y2: G.tensor_scalar(out=y2[:, :], in0=y2[:, :], scalar1=0.044715, scalar2=1.0,
                                                                     op0=ALU.mult, op1=ALU.add), reads=[y2], writes=[y2])
                            op("pool", lambda y1=y1, y2=y2: G.tensor_tensor(out=y2[:, :], in0=y2[:, :], in1=y1[:, :], op=ALU.mult),
                               reads=[y1, y2], writes=[y2])
                            op("act", lambda y2=y2: A.activation(out=y2[:, :], in_=y2[:, :], func=AF.Tanh, scale=GC),
                               reads=[y2], writes=[y2])
                            op("dve", lambda y1=y1, y2=y2: V.scalar_tensor_tensor(out=y1[:, :], in0=y2[:, :], scalar=1.0, in1=y1[:, :],
                                                                                  op0=ALU.add, op1=ALU.mult), reads=[y1, y2], writes=[y1])
                            op("pool", lambda c=c, hh=hh: G.tensor_tensor(out=hh[:, :], in0=hh[:, :], in1=lru[:, c, 1 + s:1 + s + 512],
                                                                          op=ALU.add), reads=[hh, lru], writes=[hh])
                            op("dve", lambda c=c, hh=hh, y1=y1: V.scalar_tensor_tensor(out=lru[:, c, 1 + s:1 + s + 512], in0=hh[:, :],
                                                                                       scalar=0.5, in1=y1[:, :], op0=ALU.mult,
                                                                                       op1=ALU.mult), reads=[hh, y1, lru], writes=[lru])

            passA(True)
            op("pool", lambda: G.memset(carry[:, :], 0.0), writes=[carry])
            passA(False)
            if debug:
                dma("sp", dbg["k"][:, :], kT[:, :], reads=[kT])
                dma("sp", dbg["v"][:, :], vt[:, :, :].rearrange("p a b -> p (a b)"), reads=[vt])
                dma("sp", dbg["lru"][:, :], lru[:, :, :].rearrange("p a b -> p (a b)"), reads=[lru])
            C.barrier(dma_tiles=[kT, vt, lru, scr] + xw)

        with ExitStack() as esC:
            ring = [C.tile(esC, "ring%d" % i, [128, SLOT_W], BF16) for i in range(NSLOT)]
            xw = [C.tile(esC, "xwC%d" % i, [128, 8, WCMAX], F32) for i in range(2)]
            sqF = C.tile(esC, "sqF", [128, 8, WCMAX], BF16)
            sdF = C.tile(esC, "sdF", [128, WCMAX], F32)
            rsF = C.tile(esC, "rsF", [128, WCMAX], F32)
            sqB = C.tile(esC, "sqB", [128, 8, WCMAX], BF16)
            sdB = C.tile(esC, "sdB", [128, WCMAX], F32)
            rsB = C.tile(esC, "rsB", [128, WCMAX], F32)
            xb = C.tile(esC, "xbC", [128, 8, WCMAX], BF16)
            h2 = C.tile(esC, "h2C", [128, 8, WCMAX], BF16)
            tbw = C.tile(esC, "tabC", [128, 2, WCMAX], F32)
            qT = C.tile(esC, "qT", [128, 4, WCMAX], BF16)
            attn2 = [C.tile(esC, "attn%d" % i, [128, 4, WCMAX], BF16) for i in range(2)]
            qz = C.tile(esC, "qz", [128, WCMAX], F32)
            rden = qz
            qsd = C.tile(esC, "qsd", [128, WCMAX], F32)
            qn = C.tile(esC, "qn", [128, WCMAX], F32)
            qt1 = C.tile(esC, "qt1", [128, WCMAX], F32)
            qt2 = C.tile(esC, "qt2", [128, WCMAX], F32)
            qsq = C.tile(esC, "qsq", [128, WCMAX], BF16)
            ET = [C.tile(esC, "et%d" % i, [128, 2, WCMAX], BF16) for i in range(3)]
            aall = C.tile(esC, "aall", [128, 24, SC], BF16)
            TG = [C.tile(esC, "tg%d" % i, [128, SC], F32) for i in range(2)]
            TV = [C.tile(esC, "tv%d" % i, [128, SC], F32) for i in range(2)]
            FS = [C.tile(esC, "fs%d" % i, [128, SC], F32) for i in range(2)]
            SF = S[0]
            po, pden = P[0], P[1]
            BK = [(S[1], 0), (S[1], 1), (P[2], None), (P[3], None)]

            def bk(k, lo, hi):
                t, h = BK[k]
                return t[:, h, lo:hi] if h is not None else t[:, lo:hi]

            nitems = len(ITEMS)
            item_idx = {(k, j): n for n, (k, j, w) in enumerate(ITEMS)}
            st = {"issued": 0, "used": 0, "seq": [], "dry": True, "released": set()}

            def issue_upto(n):
                seq = st["seq"]
                while st["issued"] < min(n, len(seq)):
                    g = st["issued"]
                    if g >= NSLOT and (g - NSLOT) not in st["released"]:
                        break
                    it = ITEMS[seq[g]]
                    off = ITEM_OFF[seq[g]]
                    slot = ring[g % NSLOT]
                    dma("sp", slot[:, 0:it[2]], scr_d[:, off:off + it[2]], reads=[scr], writes=[slot], semtile=slot)
                    st["issued"] += 1

            def get_item(kind, j):
                if st["dry"]:
                    st["seq"].append(item_idx[(kind, j)])
                    return ring[0], None
                g = st["used"]
                assert st["seq"][g] == item_idx[(kind, j)]
                issue_upto(g + NSLOT)
                assert st["issued"] > g
                st["used"] += 1
                return ring[g % NSLOT], g

            def release(g):
                if g is None:
                    return
                st["released"].add(g)
                issue_upto(st["used"] + NSLOT)

            def dop(e, fn, reads=(), writes=()):
                if not st["dry"]:
                    op(e, fn, reads, writes)

            def ddma(*a, **k):
                if not st["dry"]:
                    dma(*a, **k)

            def win(i):
                s = i * SC
                e = min(s + SC, T)
                return s, e, e - s, e - s + 2

            def rms_stats(X, lo, hi, Pt, sq, sd, rs):
                n = hi - lo
                dop("act", lambda: A.activation(out=sq[:, :, 0:n], in_=X[:, :, lo:hi], func=AF.Square), reads=[X], writes=[sq])

                def mm():
                    for c in range(8):
                        last = PE.matmul(Pt[:, 0:n], lhsT=ones_bf, rhs=sq[:, c, 0:n], start=(c == 0), stop=(c == 7))
                    return last
                dop("pe", mm, reads=[sq, cbf], writes=[Pt])
                dop("act", lambda: A.activation(out=sd[:, 0:n], in_=Pt[:, 0:n], func=AF.Sqrt, scale=1.0 / D, bias=EPS),
                    reads=[Pt], writes=[sd])
                dop("dve", lambda: V.reciprocal(out=rs[:, 0:n], in_=sd[:, 0:n]), reads=[sd], writes=[rs])

            def front(i):
                s, e, Wc, W = win(i)
                X = xw[i % 2]
                attn = attn2[i % 2]
                ddma("sp", X[:, :, 0:W], xv[:, :, s + 1:e + 3], writes=[X])
                ddma("sp", tbw[:, :, 0:W], tab_d[:, :, s + 1:e + 3], writes=[tbw])
                rms_stats(X, 0, W, po, sqF, sdF, rsF)
                for c in range(8):
                    dop("pool", lambda c=c: G.tensor_scalar(out=xb[:, c, 0:W], in0=X[:, c, 0:W],
                                                             scalar1=pvec[:, PV_G1 + c:PV_G1 + c + 1], scalar2=0.0,
                                                             op0=ALU.mult, op1=ALU.add), reads=[X, pvec], writes=[xb])
                yield 6.0
                for j in range(4):
                    wt, gi = get_item("wq", j)
                    pq = None

                    def q_mm(wt=wt):
                        for kc in range(8):
                            last = PE.matmul(pden[:, 0:W], lhsT=wt[:, kc * 128:(kc + 1) * 128], rhs=xb[:, kc, 0:W],
                                             start=(kc == 0), stop=(kc == 7))
                        return last
                    dop("pe", q_mm, reads=[wt, xb], writes=[pden])
                    release(gi)
                    dop("dve", lambda: V.tensor_tensor(out=qz[:, 0:W], in0=pden[:, 0:W], in1=rsF[:, 0:W], op=ALU.mult),
                        reads=[pden, rsF], writes=[qz])
                    dop("act", lambda: A.activation(out=qsq[:, 0:W], in_=qz[:, 0:W], func=AF.Square), reads=[qz], writes=[qsq])
                    dop("pe", lambda: PE.matmul(po[:, 0:W], lhsT=bd_bf, rhs=qsq[:, 0:W], start=True, stop=True),
                        reads=[cbf, qsq], writes=[po])
                    dop("act", lambda: A.activation(out=qsd[:, 0:W], in_=po[:, 0:W], func=AF.Sqrt, scale=1.0 / 64, bias=EPS),
                        reads=[po], writes=[qsd])
                    dop("dve", lambda: V.reciprocal(out=qsd[:, 0:W], in_=qsd[:, 0:W]), reads=[qsd], writes=[qsd])
                    dop("dve", lambda: V.scalar_tensor_tensor(out=qn[:, 0:W], in0=qz[:, 0:W], scalar=pvec[:, PV_GQ:PV_GQ + 1],
                                                              in1=qsd[:, 0:W], op0=ALU.mult, op1=ALU.mult),
                        reads=[qz, qsd, pvec], writes=[qn])
                    dop("pe", lambda: PE.matmul(po[:, 0:W], lhsT=R32, rhs=qn[:, 0:W], start=True, stop=True),
                        reads=[cmat, qn], writes=[po])
                    dop("pool", lambda: G.tensor_tensor(out=qt1[:, 0:W], in0=qn[:, 0:W], in1=tbw[:, 0, 0:W], op=ALU.mult),
                        reads=[qn, tbw], writes=[qt1])
                    dop("dve", lambda: V.tensor_tensor(out=qt2[:, 0:W], in0=po[:, 0:W], in1=tbw[:, 1, 0:W], op=ALU.mult),
                        reads=[po, tbw], writes=[qt2])
                    dop("dve", lambda j=j: V.tensor_tensor(out=qT[:, j, 0:W], in0=qt1[:, 0:W], in1=qt2[:, 0:W], op=ALU.add),
                        reads=[qt1, qt2], writes=[qT])
                    yield 8.0
                for j in range(4):
                    def sc_mm(kt, j=j):
                        PE.matmul(SF[:, 0, 0:W], lhsT=kT[0:64, kt * 128:(kt + 1) * 128], rhs=qT[0:64, j, 0:W], start=True, stop=True)
                        return PE.matmul(SF[:, 1, 0:W], lhsT=kT[64:128, kt * 128:(kt + 1) * 128], rhs=qT[64:128, j, 0:W],
                                         start=True, stop=True)
                    dop("pe", lambda: sc_mm(0), reads=[kT, qT], writes=[SF])
                    for kt in range(32):
                        E = ET[kt % 3]
                        dop("act", lambda E=E: A.activation(out=E[:, :, 0:W], in_=SF[:, :, 0:W], func=AF.Exp, scale=0.125),
                            reads=[SF], writes=[E])

                        def pv_mm(kt=kt, E=E):
                            f = (kt == 0)
                            l = (kt == 31)
                            PE.matmul(po[0:64, 0:W], lhsT=vt[:, kt, 0:64], rhs=E[:, 0, 0:W], start=f, stop=l)
                            PE.matmul(po[64:128, 0:W], lhsT=vt[:, kt, 64:128], rhs=E[:, 1, 0:W], start=f, stop=l)
                            PE.matmul(pden[0:64, 0:W], lhsT=ones_bf[:, 0:64], rhs=E[:, 0, 0:W], start=f, stop=l)
                            return PE.matmul(pden[64:128, 0:W], lhsT=ones_bf[:, 64:128], rhs=E[:, 1, 0:W], start=f, stop=l)
                        if kt + 1 < 32:
                            dop("pe", lambda kt=kt: sc_mm(kt + 1), reads=[kT, qT], writes=[SF])
                        dop("pe", pv_mm, reads=[vt, E, cbf], writes=[po, pden])
                        yield 1.0
                    dop("dve", lambda: V.reciprocal(out=rden[:, 0:W], in_=pden[:, 0:W]), reads=[pden], writes=[rden])
                    dop("dve", lambda j=j: V.tensor_tensor(out=attn[:, j, 0:W], in0=po[:, 0:W], in1=rden[:, 0:W], op=ALU.mult),
                        reads=[po, rden], writes=[attn])
                    yield 1.0

            def back(i):
                s, e, Wc, W = win(i)
                X = xw[i % 2]
                attn = attn2[i % 2]
                for m in range(8):
                    wt, gi = get_item("wo", m)
                    k = m % 4
                    pt = BK[k][0]
                    for g in range(2):
                        def o_mm(wt=wt, k=k, g=g):
                            for rc in range(g * 4, g * 4 + 4):
                                rhs = attn[:, rc, 0:W] if rc < 4 else lru[:, rc - 4, s:s + W]
                                last = PE.matmul(bk(k, 0, W), lhsT=wt[:, rc * 128:(rc + 1) * 128], rhs=rhs, start=(rc == 0), stop=(rc == 7))
                            return last
                        dop("pe", o_mm, reads=[wt, attn, lru], writes=[pt])
                        if g == 1:
                            release(gi)
                        yield 1.0
                    dop("dve", lambda m=m, k=k: V.tensor_tensor(out=X[:, m, 0:W], in0=bk(k, 0, W), in1=X[:, m, 0:W], op=ALU.add),
                        reads=[pt, X], writes=[X])
                rms_stats(X, 0, W, P[3], sqB, sdB, rsB)
                for m in range(8):
                    dop("dve", lambda m=m: V.scalar_tensor_tensor(out=h2[:, m, 0:W], in0=X[:, m, 0:W],
                                                                  scalar=pvec[:, PV_G2 + m:PV_G2 + m + 1], in1=rsB[:, 0:W],
                                                                  op0=ALU.mult, op1=ALU.mult), reads=[X, rsB, pvec], writes=[h2])
                if i == 0:
                    dop("pool", lambda: G.memset(h2[:, :, 0:1], 0.0), writes=[h2])
                if e == T:
                    dop("pool", lambda: G.memset(h2[:, :, W - 1:W], 0.0), writes=[h2])
                yield 3.0
                for jj in range(24):
                    wt, gi = get_item("wu", jj)
                    kg, kv_ = (0, 1) if jj % 2 == 0 else (2, 3)
                    tg_t, tv_t = BK[kg][0], BK[kv_][0]
                    for g in range(4):
                        def u_mm(wt=wt, g=g, kg=kg, kv_=kv_):
                            half = g // 2
                            k = kg if half == 0 else kv_
                            for kc in range((g % 2) * 4, (g % 2) * 4 + 4):
                                last = PE.matmul(bk(k, 0, W), lhsT=wt[:, half * 1024 + kc * 128:half * 1024 + (kc + 1) * 128],
                                                 rhs=h2[:, kc, 0:W], start=(kc == 0), stop=(kc == 7))
                            return last
                        dop("pe", u_mm, reads=[wt, h2], writes=[tg_t if g < 2 else tv_t])
                        if g == 3:
                            release(gi)
                        yield 1.0
                    tg, tv, fs = TG[jj % 2], TV[jj % 2], FS[jj % 2]
                    for (k, Bt, dstt, ch) in ((kg, tg_t, tg, jj), (kv_, tv_t, tv, 24 + jj)):
                        w0c = PV_UCW + 0 * 48 + ch
                        w1c = PV_UCW + 1 * 48 + ch
                        w2c = PV_UCW + 2 * 48 + ch
                        bc = PV_UCB + ch
                        dop("act", lambda k=k, dstt=dstt, w1c=w1c, bc=bc: A.activation(
                            out=dstt[:, 0:Wc], in_=bk(k, 1, 1 + Wc), func=AF.Identity, scale=pvec[:, w1c:w1c + 1],
                            bias=pvec[:, bc:bc + 1]), reads=[Bt, pvec], writes=[dstt])
                        dop("dve", lambda k=k, dstt=dstt, w0c=w0c: V.scalar_tensor_tensor(
                            out=dstt[:, 0:Wc], in0=bk(k, 0, Wc), scalar=pvec[:, w0c:w0c + 1], in1=dstt[:, 0:Wc],
                            op0=ALU.mult, op1=ALU.add), reads=[Bt, pvec, dstt], writes=[dstt])
                        dop("dve", lambda k=k, dstt=dstt, w2c=w2c: V.scalar_tensor_tensor(
                            out=dstt[:, 0:Wc], in0=bk(k, 2, 2 + Wc), scalar=pvec[:, w2c:w2c + 1], in1=dstt[:, 0:Wc],
                            op0=ALU.mult, op1=ALU.add), reads=[Bt, pvec, dstt], writes=[dstt])
                    dop("act", lambda tg=tg, fs=fs: A.activation(out=fs[:, 0:Wc], in_=tg[:, 0:Wc], func=AF.Square),
                        reads=[tg], writes=[fs])
                    dop("pool", lambda fs=fs: G.tensor_scalar(out=fs[:, 0:Wc], in0=fs[:, 0:Wc], scalar1=0.044715, scalar2=1.0,
                                                               op0=ALU.mult, op1=ALU.add), reads=[fs], writes=[fs])
                    dop("pool", lambda fs=fs, tg=tg: G.tensor_tensor(out=fs[:, 0:Wc], in0=fs[:, 0:Wc], in1=tg[:, 0:Wc], op=ALU.mult),
                        reads=[fs, tg], writes=[fs])
                    dop("act", lambda fs=fs: A.activation(out=fs[:, 0:Wc], in_=fs[:, 0:Wc], func=AF.Tanh, scale=GC),
                        reads=[fs], writes=[fs])
                    dop("dve", lambda fs=fs, tg=tg: V.scalar_tensor_tensor(out=tg[:, 0:Wc], in0=fs[:, 0:Wc], scalar=1.0, in1=tg[:, 0:Wc],
                                                                          op0=ALU.add, op1=ALU.mult), reads=[fs, tg], writes=[tg])
                    dop("dve", lambda jj=jj, tg=tg, tv=tv: V.scalar_tensor_tensor(out=aall[:, jj, 0:Wc], in0=tg[:, 0:Wc], scalar=0.5,
                                                                                  in1=tv[:, 0:Wc], op0=ALU.mult, op1=ALU.mult),
                        reads=[tg, tv], writes=[aall])
                for m in range(8):
                    wt, gi = get_item("wd", m)
                    k = m % 4
                    pt = BK[k][0]
                    for g in range(6):
                        def d_mm(wt=wt, k=k, g=g):
                            for jc in range(g * 4, g * 4 + 4):
                                last = PE.matmul(bk(k, 0, Wc), lhsT=wt[:, jc * 128:(jc + 1) * 128], rhs=aall[:, jc, 0:Wc],
                                                 start=(jc == 0), stop=(jc == 23))
                            return last
                        dop("pe", d_mm, reads=[wt, aall], writes=[pt])
                        if g == 5:
                            release(gi)
                        yield 1.0
                    dop("dve", lambda m=m, k=k: V.tensor_tensor(out=X[:, m, 1:1 + Wc], in0=bk(k, 0, Wc), in1=X[:, m, 1:1 + Wc],
                                                                op=ALU.add), reads=[pt, X], writes=[X])
                rms_stats(X, 1, 1 + Wc, P[3], sqB, sdB, rsB)
                for m in range(8):
                    dop("dve", lambda m=m: V.scalar_tensor_tensor(out=X[:, m, 1:1 + Wc], in0=X[:, m, 1:1 + Wc],
                                                                  scalar=pvec[:, PV_GF + m:PV_GF + m + 1], in1=rsB[:, 0:Wc],
                                                                  op0=ALU.mult, op1=ALU.mult), reads=[X, rsB, pvec], writes=[X])
                ddma("sp", ov[:, :, s:e], X[:, :, 1:1 + Wc], reads=[X], semtile=X)
                yield 3.0

            def schedule():
                def run_pair(ga, gb):
                    ta = tb_ = 0.0
                    da = ga is None
                    db = gb is None
                    while not (da and db):
                        if not da and (db or ta <= tb_):
                            try:
                                ta += next(ga)
                            except StopIteration:
                                da = True
                        else:
                            try:
                                tb_ += next(gb)
                            except StopIteration:
                                db = True
                run_pair(front(0), None)
                for i in range(nwc):
                    run_pair(front(i + 1) if i + 1 < nwc else None, back(i))

            if nwc > 0:
                st["dry"] = True
                schedule()
                st["dry"] = False
                issue_upto(NSLOT)
                schedule()
            C.barrier(dma_tiles=xw)
    return nc, dbg


def _rope_tables():
    half = 32
    inv_freq = (1.0 / (np.float32(10000.0) ** (np.arange(0, half, 2, dtype=np.float32) / np.float32(half)))).astype(np.float32)
    t = np.arange(T)
    row = (t // 64).astype(np.float32)
    col = (t % 64).astype(np.float32)
    tab = np.zeros((128, 2, T + 4), np.float32)
    for p in range(128):
        d = p % 64
        pos = row if d < 32 else col
        w = d % 32
        fi = w % 16
        ang = (pos * inv_freq[fi]).astype(np.float32)
        tab[p, 0, 2:T + 2] = np.cos(ang)
        sn = np.sin(ang)
        tab[p, 1, 2:T + 2] = -sn if w < 16 else sn
    return tab


def _consts():
    cm = np.zeros((128, 512), np.float32)
    cm[:, 0:128] = 1.0
    cm[0:64, 128:192] = 1.0
    cm[64:128, 192:256] = 1.0
    for m in range(128):
        w = (m % 64) % 32
        partner = m + 16 if w < 16 else m - 16
        cm[partner, 256 + m] = 1.0
    cm[:, 384:512] = np.eye(128, dtype=np.float32)
    return cm


def _host_prep(inp):
    f = lambda a: np.ascontiguousarray(np.asarray(a, dtype=np.float32))
    perm = []
    for j in range(4):
        perm += list(range(j * 64, (j + 1) * 64)) + list(range((4 + j) * 64, (5 + j) * 64))
    perm = np.array(perm)
    w_in = f(inp["w_in"])[0]
    wq = np.ascontiguousarray(w_in[:, 0:512][:, perm])
    wkvl = np.ascontiguousarray(w_in[:, 512:1792])
    w_out = f(inp["w_out"])[0]
    wo = np.ascontiguousarray(np.concatenate([w_out[0:512][perm], w_out[512:]], axis=0))
    wu = f(inp["w_up"])[0]
    wd = f(inp["w_down"])[0]
    pv = np.zeros((128, PV_N), np.float32)

    def chunked(v, n):
        return np.asarray(v, np.float32).reshape(n, 128).T
    pv[:, PV_G1:PV_G1 + 8] = chunked(inp["norm1_g"][0], 8)
    pv[:, PV_G2:PV_G2 + 8] = chunked(inp["norm2_g"][0], 8)
    pv[:, PV_GF:PV_GF + 8] = chunked(inp["final_g"], 8)
    pv[:, PV_GQ] = np.tile(np.asarray(inp["q_norm_g"][0], np.float32), 2)
    pv[:, PV_GK] = np.tile(np.asarray(inp["k_norm_g"][0], np.float32), 2)
    lcw = np.asarray(inp["lru_conv_w"][0], np.float32)
    for c in range(4):
        for j in range(4):
            pv[:, PV_LCW + c * 4 + j] = lcw[j, c * 128:(c + 1) * 128]
    pv[:, PV_LCB:PV_LCB + 4] = chunked(inp["lru_conv_b"][0], 4)
    pv[:, PV_BAF:PV_BAF + 4] = chunked(np.asarray(inp["ba_f"][0]).reshape(-1), 4)
    pv[:, PV_BXF:PV_BXF + 4] = chunked(np.asarray(inp["bx_f"][0]).reshape(-1), 4)
    pv[:, PV_BAB:PV_BAB + 4] = chunked(np.asarray(inp["ba_b"][0]).reshape(-1), 4)
    pv[:, PV_BXB:PV_BXB + 4] = chunked(np.asarray(inp["bx_b"][0]).reshape(-1), 4)
    pv[:, PV_LAMF:PV_LAMF + 4] = chunked(inp["lam_f"][0], 4)
    pv[:, PV_LAMB:PV_LAMB + 4] = chunked(inp["lam_b"][0], 4)
    ucw = np.asarray(inp["up_conv_w"][0], np.float32)
    for tap in range(3):
        pv[:, PV_UCW + tap * 48:PV_UCW + (tap + 1) * 48] = chunked(ucw[tap], 48)
    pv[:, PV_UCB:PV_UCB + 48] = chunked(inp["up_conv_b"][0], 48)
    gate = np.zeros((128, 16 * 128), np.float32)
    for kind, name in enumerate(("wa_f", "wx_f", "wa_b", "wx_b")):
        w = np.asarray(inp[name][0], np.float32)
        for c in range(4):
            base = (kind * 4 + c) * 128
            gate[0:64, base:base + 64] = w[2 * c]
            gate[64:128, base + 64:base + 128] = w[2 * c + 1]
    shared = {"wq": wq, "wkvl": wkvl, "wo": wo, "wu": wu, "wd": wd, "pvec": pv, "gatew": gate,
              "cmat": _consts(), "tab": _rope_tables()}
    x = np.asarray(inp["x"], np.float32)
    maps = []
    for b in range(x.shape[0]):
        xT = np.zeros((D, T + 4), np.float32)
        xT[:, 2:T + 2] = x[b].T
        m = dict(shared)
        m["xT"] = xT
        maps.append(m)
    return maps


_CACHE = {}


def kernel(**inputs):
    maps = _host_prep(inputs)
    if "nc" not in _CACHE:
        _CACHE["nc"] = _build()[0]
    nc = _CACHE["nc"]
    res = run_bass_kernel_spmd(nc, maps, core_ids=list(range(len(maps))))
    out = np.stack([np.ascontiguousarray(r["outT"].T) for r in res.results], axis=0)
    return out.astype(np.float32)
```

```python
import numpy as np
from contextlib import ExitStack
import concourse.bass as bass
import concourse.mybir as mybir
from concourse.bass_utils import run_bass_kernel_spmd

F32 = mybir.dt.float32
BF16 = mybir.dt.bfloat16
AF = mybir.ActivationFunctionType
ALU = mybir.AluOpType

T = 4096
D = 1024
EPS = 1e-6
NWA = 8
XWA = 516
SC = 456
NWC = 9
WCMAX = SC + 2
GC = 0.7978845608028654
PV_G1, PV_G2, PV_GF, PV_GQ, PV_GK = 0, 8, 16, 24, 25
PV_LCW, PV_LCB = 26, 42
PV_BAF, PV_BXF, PV_BAB, PV_BXB = 46, 50, 54, 58
PV_LAMF, PV_LAMB = 62, 66
PV_UCW, PV_UCB = 70, 214
PV_N = 262
DV_HBAF, DV_HBXF, DV_HBAB, DV_HBXB = 0, 4, 8, 12
DV_CLF, DV_CLB, DV_HCLF, DV_HCLB = 16, 20, 24, 28
DV_N = 32
ITEMS = [("wq", j, 1024) for j in range(4)] + [("wo", m, 1024) for m in range(8)] + \
        [("wu", j, 2048) for j in range(24)] + [("wd", m, 3072) for m in range(8)]
ITEM_OFF = []
_o = 0
for _it in ITEMS:
    ITEM_OFF.append(_o)
    _o += _it[2]
SCR_W = _o
NSLOT = 4
SLOT_W = 3072


class Tl:
    __slots__ = ("t", "w", "r", "sem", "cnt", "name")

    def __init__(self, t, name):
        self.t = t
        self.w = None
        self.r = []
        self.sem = None
        self.cnt = 0
        self.name = name

    def __getitem__(self, k):
        return self.t[k]


class Ctx:
    def __init__(self, nc, es):
        self.nc = nc
        self.es = es
        self.eng = {"pe": nc.tensor, "act": nc.scalar, "dve": nc.vector, "pool": nc.gpsimd, "sp": nc.sync}
        self.sem = {k: es.enter_context(nc.semaphore("s_" + k)) for k in ("pe", "act", "dve", "pool")}
        self.cnt = {k: 0 for k in ("pe", "act", "dve", "pool")}
        self.seen = {k: {} for k in self.eng}
        self.nsem = 0

    def tile(self, es, name, shape, dt):
        return Tl(es.enter_context(self.nc.sbuf_tensor("t_" + name, shape, dt)), name)

    def ptile(self, es, name, shape, dt=F32):
        return Tl(es.enter_context(self.nc.psum_tensor("t_" + name, shape, dt)), name)

    def dram(self, name):
        return Tl(None, name)

    def _dsem(self, tl):
        if tl.sem is None:
            tl.sem = self.es.enter_context(self.nc.semaphore("d%d_%s" % (self.nsem, tl.name)))
            self.nsem += 1
        return tl.sem

    def _wait(self, e, deps):
        seen = self.seen[e]
        need = {}
        for d in deps:
            if d is None:
                continue
            if d[0] == "dma":
                key, val = ("dma", id(d[1])), d[2]
                semh = d[1]
            else:
                if d[0] == e and e in ("pe", "sp"):
                    continue
                key, val = d[0], d[1]
                semh = self.sem[d[0]]
            if seen.get(key, 0) >= val:
                continue
            if key not in need or need[key][1] < val:
                need[key] = (semh, val)
        for key, (semh, val) in need.items():
            self.eng[e].wait_ge(semh, val)
            seen[key] = val

    def _deps(self, reads, writes):
        deps = []
        for t in reads:
            deps.append(t.w)
        for t in writes:
            deps.append(t.w)
            deps.extend(t.r)
        return deps

    def op(self, e, fn, reads=(), writes=()):
        self._wait(e, self._deps(reads, writes))
        ins = fn()
        self.cnt[e] += 1
        ins.then_inc(self.sem[e], 1)
        tag = (e, self.cnt[e])
        for t in writes:
            t.w = tag
            t.r = []
        for t in reads:
            if t.w is not None and t.w == tag:
                continue
            t.r.append(tag)
        return ins

    def dma(self, q, out_ap, in_ap, reads=(), writes=(), semtile=None):
        self._wait(q, self._deps(reads, writes))
        st = semtile if semtile is not None else (writes[0] if writes else reads[0])
        semh = self._dsem(st)
        st.cnt += 16
        self.eng[q].dma_start(out=out_ap, in_=in_ap).then_inc(semh, 16)
        tag = ("dma", semh, st.cnt)
        for t in writes:
            t.w = tag
            t.r = []
        for t in reads:
            t.r.append(tag)

    def barrier(self, dma_tiles=()):
        for e in ("pe", "act", "dve", "pool", "sp"):
            deps = [(o, self.cnt[o]) for o in ("pe", "act", "dve", "pool") if self.cnt[o] > 0]
            for t in dma_tiles:
                if t.sem is not None:
                    deps.append(("dma", t.sem, t.cnt))
            self._wait(e, deps)


def _build(debug=False, nwc=NWC):
    nc = bass.Bass("TRN2", target_bir_lowering=False)
    dt = nc.dram_tensor
    x_d = dt("xT", [D, T + 4], F32, kind="ExternalInput").ap()
    wq_d = dt("wq", [D, 512], F32, kind="ExternalInput").ap()
    wkvl_d = dt("wkvl", [D, 1280], F32, kind="ExternalInput").ap()
    wo_d = dt("wo", [D, D], F32, kind="ExternalInput").ap()
    wu_d = dt("wu", [D, 6144], F32, kind="ExternalInput").ap()
    wd_d = dt("wd", [3072, D], F32, kind="ExternalInput").ap()
    pvec_d = dt("pvec", [128, PV_N], F32, kind="ExternalInput").ap()
    gatew_d = dt("gatew", [128, 16 * 128], F32, kind="ExternalInput").ap()
    cmat_d = dt("cmat", [128, 4 * 128], F32, kind="ExternalInput").ap()
    tab_d = dt("tab", [128, 2, T + 4], F32, kind="ExternalInput").ap()
    out_d = dt("outT", [D, T], F32, kind="ExternalOutput").ap()
    scr_d = dt("wscr", [128, SCR_W], BF16).ap()
    dbg = {}
    if debug:
        dbg["k"] = dt("dbg_k", [128, T], BF16, kind="ExternalOutput").ap()
        dbg["v"] = dt("dbg_v", [128, 32 * 128], BF16, kind="ExternalOutput").ap()
        dbg["lru"] = dt("dbg_lru", [128, 4 * (T + 2)], BF16, kind="ExternalOutput").ap()

    xv = x_d.rearrange("(c p) t -> p c t", p=128)
    ov = out_d.rearrange("(c p) t -> p c t", p=128)

    with ExitStack() as es:
        C = Ctx(nc, es)
        op, dma = C.op, C.dma
        V = nc.vector
        A = nc.scalar
        G = nc.gpsimd
        PE = nc.tensor

        pvec = C.tile(es, "pvec", [128, PV_N], F32)
        dv = C.tile(es, "dv", [128, DV_N], F32)
        cmat = C.tile(es, "cmat", [128, 512], F32)
        cbf = C.tile(es, "cbf", [128, 512], BF16)
        gw = C.tile(es, "gw", [128, 16 * 128], BF16)
        kT = C.tile(es, "kT", [128, T], BF16)
        vt = C.tile(es, "vt", [128, 32, 128], BF16)
        lru = C.tile(es, "lru", [128, 4, T + 2], BF16)
        carry = C.tile(es, "carry", [128, 4], F32)
        S = [C.ptile(es, "S%d" % i, [128, 2, 512]) for i in range(2)]
        P = [C.ptile(es, "P%d" % i, [128, 512]) for i in range(4)]
        scr = C.dram("scr")
        ones_bf = cbf[:, 0:128]
        bd_bf = cbf[:, 128:256]
        ident_bf = cbf[:, 384:512]
        R32 = cmat[:, 256:384]

        dma("sp", pvec[:, :], pvec_d[:, :], writes=[pvec])
        dma("sp", cmat[:, :], cmat_d[:, :], writes=[cmat])
        op("pool", lambda: G.tensor_copy(out=cbf[:, :], in_=cmat[:, :]), reads=[cmat], writes=[cbf])
        op("pool", lambda: G.memset(lru[:, :, :], 0.0), writes=[lru])
        op("pool", lambda: G.memset(carry[:, :], 0.0), writes=[carry])
        for (src, dst) in ((PV_BAF, DV_HBAF), (PV_BXF, DV_HBXF), (PV_BAB, DV_HBAB), (PV_BXB, DV_HBXB)):
            op("dve", lambda src=src, dst=dst: V.tensor_scalar(
                out=dv[:, dst:dst + 4], in0=pvec[:, src:src + 4], scalar1=0.5, scalar2=None, op0=ALU.mult),
               reads=[pvec], writes=[dv])
        with ExitStack() as es0:
            et0 = C.tile(es0, "s_e", [128, 8], F32)
            e2 = C.tile(es0, "s_e2", [128, 8], F32)
            acc = C.tile(es0, "s_acc", [128, 8], F32)
            op("act", lambda: A.activation(out=et0[:, :], in_=pvec[:, PV_LAMF:PV_LAMF + 8], func=AF.Exp, scale=-1.0),
               reads=[pvec], writes=[et0])
            op("dve", lambda: V.tensor_scalar(out=acc[:, :], in0=et0[:, :], scalar1=-0.25, scalar2=1.0 / 3.0,
                                              op0=ALU.mult, op1=ALU.add), reads=[et0], writes=[acc])
            op("dve", lambda: V.tensor_tensor(out=acc[:, :], in0=acc[:, :], in1=et0[:, :], op=ALU.mult),
               reads=[et0, acc], writes=[acc])
            op("dve", lambda: V.tensor_scalar(out=acc[:, :], in0=acc[:, :], scalar1=-0.5, scalar2=None, op0=ALU.add),
               reads=[acc], writes=[acc])
            op("dve", lambda: V.tensor_tensor(out=acc[:, :], in0=acc[:, :], in1=et0[:, :], op=ALU.mult),
               reads=[et0, acc], writes=[acc])
            op("dve", lambda: V.tensor_scalar(out=acc[:, :], in0=acc[:, :], scalar1=1.0, scalar2=None, op0=ALU.add),
               reads=[acc], writes=[acc])
            op("dve", lambda: V.tensor_tensor(out=e2[:, :], in0=acc[:, :], in1=et0[:, :], op=ALU.mult),
               reads=[et0, acc], writes=[e2])
            op("dve", lambda: V.tensor_scalar(out=dv[:, DV_CLF:DV_CLF + 8], in0=e2[:, :], scalar1=-8.0, scalar2=None,
                                              op0=ALU.mult), reads=[e2], writes=[dv])
            op("dve", lambda: V.tensor_scalar(out=dv[:, DV_HCLF:DV_HCLF + 8], in0=e2[:, :], scalar1=-4.0, scalar2=None,
                                              op0=ALU.mult), reads=[e2], writes=[dv])
            C.barrier()

        with ExitStack() as esA:
            wk = C.tile(esA, "wkvl", [128, 8, 1280], BF16)
            dma("pool", gw[:, :], gatew_d[:, :], writes=[gw])
            wkv = wkvl_d.rearrange("(c p) n -> p c n", p=128)
            for c in range(8):
                dma("pool", wk[:, c, :], wkv[:, c, :], writes=[wk])
            wqv = wq_d.rearrange("(c p) n -> p c n", p=128)
            wov = wo_d.rearrange("(c p) n -> p c n", p=128)
            wuv = wu_d.rearrange("(c p) n -> p c n", p=128)
            wdv = wd_d.rearrange("(c p) n -> p c n", p=128)
            scr_sem = C._dsem(scr)

            def cast(dst, src):
                nc.gpsimd.dma_start(out=dst, in_=src).then_inc(scr_sem, 16)
                scr.cnt += 16
            for it, off in zip(ITEMS, ITEM_OFF):
                kind, j, w = it
                if kind == "wq":
                    cast(scr_d[:, off:off + w].rearrange("p (c n) -> p c n", c=8), wqv[:, :, j * 128:(j + 1) * 128])
                elif kind == "wo":
                    cast(scr_d[:, off:off + w].rearrange("p (c n) -> p c n", c=8), wov[:, :, j * 128:(j + 1) * 128])
                elif kind == "wu":
                    cast(scr_d[:, off:off + 1024].rearrange("p (c n) -> p c n", c=8), wuv[:, :, j * 128:(j + 1) * 128])
                    cast(scr_d[:, off + 1024:off + 2048].rearrange("p (c n) -> p c n", c=8),
                         wuv[:, :, 3072 + j * 128:3072 + (j + 1) * 128])
                else:
                    cast(scr_d[:, off:off + w].rearrange("p (c n) -> p c n", c=24), wdv[:, :, j * 128:(j + 1) * 128])
            scr.w = ("dma", scr_sem, scr.cnt)

            xw = [C.tile(esA, "xwA%d" % i, [128, 8, XWA], F32) for i in range(2)]
            sq = C.tile(esA, "sqA", [128, 8, XWA], BF16)
            xb = C.tile(esA, "xbA", [128, 8, XWA], BF16)
            sd = C.tile(esA, "sdA", [128, XWA], F32)
            rs = C.tile(esA, "rsA", [128, XWA], F32)
            xr = C.tile(esA, "xrA", [128, 4, XWA], F32)
            tb = C.tile(esA, "tabA", [128, 2, 512], F32)
            gen = [C.tile(esA, "gen%d" % i, [128, 512], F32) for i in range(5)]
            ksq = C.tile(esA, "ksq", [128, 512], BF16)
            vzb = C.tile(esA, "vzb", [128, 512], BF16)
            XC = [C.tile(esA, "xc%d" % i, [128, 512], F32) for i in range(4)]
            UU = [C.tile(esA, "uu%d" % i, [128, 512], F32) for i in range(4)]
            A2 = [C.tile(esA, "a2%d" % i, [128, 512], F32) for i in range(4)]
            AA = [C.tile(esA, "aa%d" % i, [128, 512], F32) for i in range(4)]
            TR = [C.tile(esA, "tr%d" % i, [128, 512], F32) for i in range(2)]
            HH = [C.tile(esA, "hh%d" % i, [128, 512], F32) for i in range(2)]
            XCB = [C.tile(esA, "xcb%d" % i, [128, 512], BF16) for i in range(2)]

            def load_xA(i, slot):
                s = i * 512
                dma("sp", xw[slot][:, :, 0:515], xv[:, :, s:s + 515], writes=[xw[slot]])

            def passA(fwd):
                order = list(range(NWA)) if fwd else list(range(NWA - 1, -1, -1))
                load_xA(order[0], 0)
                kw = 0 if fwd else 2
                hba = DV_HBAF if fwd else DV_HBAB
                hbx = DV_HBXF if fwd else DV_HBXB
                cl = DV_CLF if fwd else DV_CLB
                hcl = DV_HCLF if fwd else DV_HCLB
                for n, i in enumerate(order):
                    slot = n % 2
                    s = i * 512
                    X = xw[slot]
                    if n + 1 < len(order):
                        load_xA(order[n + 1], (n + 1) % 2)
                    if fwd:
                        dma("sp", tb[:, :, :], tab_d[:, :, s + 2:s + 514], writes=[tb])
                    op("act", lambda: A.activation(out=sq[:, :, 0:515], in_=X[:, :, 0:515], func=AF.Square),
                       reads=[X], writes=[sq])

                    def ss_mm():
                        for c in range(8):
                            PE.matmul(P[0][:, 0:512], lhsT=ones_bf, rhs=sq[:, c, 0:512], start=(c == 0), stop=(c == 7))
                        for c in range(8):
                            last = PE.matmul(P[1][:, 0:4], lhsT=ones_bf, rhs=sq[:, c, 511:515], start=(c == 0), stop=(c == 7))
                        return last
                    op("pe", ss_mm, reads=[sq, cbf], writes=[P[0], P[1]])
                    op("act", lambda: A.activation(out=sd[:, 0:512], in_=P[0][:, 0:512], func=AF.Sqrt, scale=1.0 / D, bias=EPS),
                       reads=[P[0]], writes=[sd])
                    op("act", lambda: A.activation(out=sd[:, 512:515], in_=P[1][:, 1:4], func=AF.Sqrt, scale=1.0 / D, bias=EPS),
                       reads=[P[1], sd], writes=[sd])
                    op("dve", lambda: V.reciprocal(out=rs[:, 0:515], in_=sd[:, 0:515]), reads=[sd], writes=[rs])
                    for c in range(8):
                        op("pool", lambda c=c: G.tensor_scalar(out=xb[:, c, 0:515], in0=X[:, c, 0:515],
                                                                scalar1=pvec[:, PV_G1 + c:PV_G1 + c + 1], scalar2=0.0,
                                                                op0=ALU.mult, op1=ALU.add),
                           reads=[X, pvec], writes=[xb])
                    for c in range(4):
                        col0 = 256 + c * 128
                        pm = P[2 + (c % 2)]

                        def xr_mm(col0=col0, pm=pm):
                            for kc in range(8):
                                last = PE.matmul(pm[:, 0:512], lhsT=wk[:, kc, col0:col0 + 128], rhs=xb[:, kc, 0:512],
                                                 start=(kc == 0), stop=(kc == 7))
                            return last
                        op("pe", xr_mm, reads=[wk, xb], writes=[pm])
                        op("dve", lambda c=c, pm=pm: V.tensor_tensor(out=xr[:, c, 0:512], in0=pm[:, 0:512], in1=rs[:, 0:512],
                                                                     op=ALU.mult), reads=[pm, rs], writes=[xr])

                    def xrh_mm():
                        for c in range(4):
                            col0 = 256 + c * 128
                            for kc in range(8):
                                last = PE.matmul(P[1][:, 16 + c * 4:16 + c * 4 + 4], lhsT=wk[:, kc, col0:col0 + 128],
                                                 rhs=xb[:, kc, 511:515], start=(kc == 0), stop=(kc == 7), skip_group_check=True)
                        return last
                    op("pe", xrh_mm, reads=[wk, xb], writes=[P[1]])
                    for c in range(4):
                        op("dve", lambda c=c: V.tensor_tensor(out=xr[:, c, 512:515], in0=P[1][:, 16 + c * 4 + 1:16 + c * 4 + 4],
                                                              in1=rs[:, 512:515], op=ALU.mult), reads=[P[1], rs], writes=[xr])
                    if fwd:
                        kz, ksd, kn, kt1, kt2 = gen

                        def k_mm():
                            for kc in range(8):
                                last = PE.matmul(P[2][:, 0:512], lhsT=wk[:, kc, 0:128], rhs=xb[:, kc, 2:514],
                                                 start=(kc == 0), stop=(kc == 7))
                            return last
                        op("pe", k_mm, reads=[wk, xb], writes=[P[2]])
                        op("dve", lambda: V.tensor_tensor(out=kz[:, :], in0=P[2][:, :], in1=rs[:, 2:514], op=ALU.mult),
                           reads=[P[2], rs], writes=[kz])
                        op("act", lambda: A.activation(out=ksq[:, :], in_=kz[:, :], func=AF.Square), reads=[kz], writes=[ksq])
                        op("pe", lambda: PE.matmul(P[2][:, :], lhsT=bd_bf, rhs=ksq[:, :], start=True, stop=True),
                           reads=[cbf, ksq], writes=[P[2]])
                        op("act", lambda: A.activation(out=ksd[:, :], in_=P[2][:, :], func=AF.Sqrt, scale=1.0 / 64, bias=EPS),
                           reads=[P[2]], writes=[ksd])
                        op("dve", lambda: V.reciprocal(out=ksd[:, :], in_=ksd[:, :]), reads=[ksd], writes=[ksd])
                        op("dve", lambda: V.scalar_tensor_tensor(out=kn[:, :], in0=kz[:, :], scalar=pvec[:, PV_GK:PV_GK + 1],
                                                                 in1=ksd[:, :], op0=ALU.mult, op1=ALU.mult),
                           reads=[kz, ksd, pvec], writes=[kn])
                        op("pe", lambda: PE.matmul(P[2][:, :], lhsT=R32, rhs=kn[:, :], start=True, stop=True),
                           reads=[cmat, kn], writes=[P[2]])
                        op("pool", lambda: G.tensor_tensor(out=kt1[:, :], in0=kn[:, :], in1=tb[:, 0, :], op=ALU.mult),
                           reads=[kn, tb], writes=[kt1])
                        op("dve", lambda: V.tensor_tensor(out=kt2[:, :], in0=P[2][:, :], in1=tb[:, 1, :], op=ALU.mult),
                           reads=[P[2], tb], writes=[kt2])
                        op("dve", lambda: V.tensor_tensor(out=kT[:, s:s + 512], in0=kt1[:, :], in1=kt2[:, :], op=ALU.add),
                           reads=[kt1, kt2], writes=[kT])

                        def v_mm():
                            for kc in range(8):
                                last = PE.matmul(P[3][:, 0:512], lhsT=wk[:, kc, 128:256], rhs=xb[:, kc, 2:514],
                                                 start=(kc == 0), stop=(kc == 7))
                            return last
                        op("pe", v_mm, reads=[wk, xb], writes=[P[3]])
                        op("dve", lambda: V.tensor_tensor(out=vzb[:, :], in0=P[3][:, :], in1=rs[:, 2:514], op=ALU.mult),
                           reads=[P[3], rs], writes=[vzb])
                        pvb = P[3][:, 0:256].bitcast(BF16)

                        def v_tr():
                            for q in range(4):
                                last = PE.transpose(out=pvb[:, q * 128:(q + 1) * 128], in_=vzb[:, q * 128:(q + 1) * 128],
                                                    identity=ident_bf)
                            return last
                        op("pe", v_tr, reads=[vzb, cbf], writes=[P[3]])
                        op("act", lambda: A.activation(out=vt[:, i * 4:i * 4 + 4, :], in_=pvb.rearrange("p (q n) -> p q n", q=4),
                                                       func=AF.Copy), reads=[P[3]], writes=[vt])
                    for c in range(4):
                        w0 = PV_LCW + c * 4
                        op("act", lambda c=c, w0=w0: A.activation(out=XC[c][:, :], in_=xr[:, c, 0:512], func=AF.Identity,
                                                                  scale=pvec[:, w0:w0 + 1], bias=pvec[:, PV_LCB + c:PV_LCB + c + 1]),
                           reads=[xr, pvec], writes=[XC[c]])
                        for j in range(1, 4):
                            op("dve", lambda c=c, j=j, w0=w0: V.scalar_tensor_tensor(
                                out=XC[c][:, :], in0=xr[:, c, j:j + 512], scalar=pvec[:, w0 + j:w0 + j + 1], in1=XC[c][:, :],
                                op0=ALU.mult, op1=ALU.add), reads=[xr, pvec, XC[c]], writes=[XC[c]])
                    for c in range(4):
                        xcb = XCB[c % 2]
                        trc = TR[c % 2]
                        Sg = S[c % 2]
                        op("pool", lambda c=c, xcb=xcb: G.tensor_copy(out=xcb[:, :], in_=XC[c][:, :]), reads=[XC[c]], writes=[xcb])

                        def g_mm(c=c, xcb=xcb, Sg=Sg):
                            PE.matmul(Sg[:, 0, :], lhsT=gw[:, (kw * 4 + c) * 128:(kw * 4 + c + 1) * 128], rhs=xcb[:, :],
                                      start=True, stop=True)
                            return PE.matmul(Sg[:, 1, :], lhsT=gw[:, ((kw + 1) * 4 + c) * 128:((kw + 1) * 4 + c + 1) * 128],
                                             rhs=xcb[:, :], start=True, stop=True)
                        op("pe", g_mm, reads=[gw, xcb], writes=[Sg])
                        op("act", lambda c=c, trc=trc, Sg=Sg: A.activation(out=trc[:, :], in_=Sg[:, 0, :], func=AF.Tanh, scale=0.5,
                                                                           bias=dv[:, hba + c:hba + c + 1]), reads=[Sg, dv], writes=[trc])
                        op("act", lambda c=c, Sg=Sg: A.activation(out=UU[c][:, :], in_=Sg[:, 1, :], func=AF.Tanh, scale=0.5,
                                                                  bias=dv[:, hbx + c:hbx + c + 1]), reads=[Sg, dv], writes=[UU[c]])
                        op("act", lambda c=c, trc=trc: A.activation(out=AA[c][:, :], in_=trc[:, :], func=AF.Exp,
                                                                    scale=dv[:, hcl + c:hcl + c + 1], bias=dv[:, hcl + c:hcl + c + 1]),
                           reads=[trc, dv], writes=[AA[c]])
                        op("act", lambda c=c, trc=trc: A.activation(out=A2[c][:, :], in_=trc[:, :], func=AF.Exp,
                                                                    scale=dv[:, cl + c:cl + c + 1], bias=dv[:, cl + c:cl + c + 1]),
                           reads=[trc, dv], writes=[A2[c]])
                        op("dve", lambda c=c: V.scalar_tensor_tensor(out=UU[c][:, :], in0=UU[c][:, :], scalar=1.0, in1=XC[c][:, :],
                                                                     op0=ALU.add, op1=ALU.mult), reads=[UU[c], XC[c]], writes=[UU[c]])
                    if not fwd:
                        Y1 = [gen[0], gen[1]]
                        Y2 = [gen[2], gen[3]]
                    for c in range(4):
                        op("act", lambda c=c: A.activation(out=A2[c][:, :], in_=A2[c][:, :], func=AF.Sqrt, scale=-1.0, bias=1.0),
                           reads=[A2[c]], writes=[A2[c]])
                    for c in range(4):
                        hh = HH[c % 2]
                        first = (fwd and i == 0) or ((not fwd) and i == NWA - 1)
                        if first:
                            col = 0 if fwd else 511
                            op("pool", lambda c=c, col=col: G.memset(A2[c][:, col:col + 1], 1.0), writes=[A2[c]])
                        op("dve", lambda c=c: V.scalar_tensor_tensor(out=UU[c][:, :], in0=UU[c][:, :], scalar=0.5, in1=A2[c][:, :],
                                                                     op0=ALU.mult, op1=ALU.mult), reads=[UU[c], A2[c]], writes=[UU[c]])
                        if fwd:
                            op("dve", lambda c=c, hh=hh: V.tensor_tensor_scan(out=hh[:, :], data0=AA[c][:, :], data1=UU[c][:, :],
                                                                              initial=carry[:, c:c + 1], op0=ALU.mult, op1=ALU.add),
                               reads=[AA[c], UU[c], carry], writes=[hh])
                            op("pool", lambda c=c, hh=hh: G.tensor_copy(out=carry[:, c:c + 1], in_=hh[:, 511:512]),
                               reads=[hh], writes=[carry])
                            op("pool", lambda c=c, hh=hh: G.tensor_copy(out=lru[:, c, 1 + s:1 + s + 512], in_=hh[:, :]),
                               reads=[hh], writes=[lru])
                        else:
                            op("dve", lambda c=c, hh=hh: V.tensor_tensor_scan(out=hh[:, ::-1], data0=AA[c][:, ::-1],
                                                                              data1=UU[c][:, ::-1], initial=carry[:, c:c + 1],
                                                                              op0=ALU.mult, op1=ALU.add),
                               reads=[AA[c], UU[c], carry], writes=[hh])
                            op("pool", lambda c=c, hh=hh: G.tensor_copy(out=carry[:, c:c + 1], in_=hh[:, 0:1]),
                               reads=[hh], writes=[carry])
                            y1, y2 = Y1[c % 2], Y2[c % 2]
                            pm = P[2 + (c % 2)]
                            col0 = 768 + c * 128

                            def y_mm(col0=col0, pm=pm):
                                for kc in range(8):
                                    last = PE.matmul(pm[:, 0:512], lhsT=wk[:, kc, col0:col0 + 128], rhs=xb[:, kc, 2:514],
                                                     start=(kc == 0), stop=(kc == 7))
                                return last
                            op("pe", y_mm, reads=[wk, xb], writes=[pm])
                            op("dve", lambda y1=y1, pm=pm: V.tensor_tensor(out=y1[:, :], in0=pm[:, :], in1=rs[:, 2:514], op=ALU.mult),
                               reads=[pm, rs], writes=[y1])
                            op("pool", lambda y1=y1, y2=y2: G.tensor_tensor(out=y2[:, :], in0=y1[:, :], in1=y1[:, :], op=ALU.mult),
                               reads=[y1], writes=[y2])
                            op("pool", lambda y2=y2: G.tensor_scalar(out=y2[:, :], in0=y2[:, :], scalar1=0.044715, scalar2=1.0,
                                                                     op0=ALU.mult, op1=ALU.add), reads=[y2], writes=[y2])
                            op("pool", lambda y1=y1, y2=y2: G.tensor_tensor(out=y2[:, :], in0=y2[:, :], in1=y1[:, :], op=ALU.mult),
                               reads=[y1, y2], writes=[y2])
                            op("act", lambda y2=y2: A.activation(out=y2[:, :], in_=y2[:, :], func=AF.Tanh, scale=GC),
                               reads=[y2], writes=[y2])
                            op("dve", lambda y1=y1, y2=y2: V.scalar_tensor_tensor(out=y1[:, :], in0=y2[:, :], scalar=1.0, in1=y1[:, :],
                                                                                  op0=ALU.add, op1=ALU.mult), reads=[y1, y2], writes=[y1])
                            op("pool", lambda c=c, hh=hh: G.tensor_tensor(out=hh[:, :], in0=hh[:, :], in1=lru[:, c, 1 + s:1 + s + 512],
                                                                          op=ALU.add), reads=[hh, lru], writes=[hh])
                            op("dve", lambda c=c, hh=hh, y1=y1: V.scalar_tensor_tensor(out=lru[:, c, 1 + s:1 + s + 512], in0=hh[:, :],
                                                                                       scalar=0.5, in1=y1[:, :], op0=ALU.mult,
                                                                                       op1=ALU.mult), reads=[hh, y1, lru], writes=[lru])

            passA(True)
            op("pool", lambda: G.memset(carry[:, :], 0.0), writes=[carry])
            passA(False)
            if debug:
                dma("sp", dbg["k"][:, :], kT[:, :], reads=[kT])
                dma("sp", dbg["v"][:, :], vt[:, :, :].rearrange("p a b -> p (a b)"), reads=[vt])
                dma("sp", dbg["lru"][:, :], lru[:, :, :].rearrange("p a b -> p (a b)"), reads=[lru])
            C.barrier(dma_tiles=[kT, vt, lru, scr] + xw)

        with ExitStack() as esC:
            ring = [C.tile(esC, "ring%d" % i, [128, SLOT_W], BF16) for i in range(NSLOT)]
            xw = [C.tile(esC, "xwC%d" % i, [128, 8, WCMAX], F32) for i in range(2)]
            sqF = C.tile(esC, "sqF", [128, 8, WCMAX], BF16)
            sdF = C.tile(esC, "sdF", [128, WCMAX], F32)
            rsF = C.tile(esC, "rsF", [128, WCMAX], F32)
            sqB = sqF
            sdB = C.tile(esC, "sdB", [128, WCMAX], F32)
            rsB = C.tile(esC, "rsB", [128, WCMAX], F32)
            xb = C.tile(esC, "xbC", [128, 8, WCMAX], BF16)
            h2 = C.tile(esC, "h2C", [128, 8, WCMAX], BF16)
            tbw = C.tile(esC, "tabC", [128, 2, WCMAX], F32)
            qT = C.tile(esC, "qT", [128, 4, WCMAX], BF16)
            attn2 = [C.tile(esC, "attn%d" % i, [128, 4, WCMAX], BF16) for i in range(2)]
            qz = C.tile(esC, "qz", [128, WCMAX], F32)
            rden = qz
            qsd = C.tile(esC, "qsd", [128, WCMAX], F32)
            qn = C.tile(esC, "qn", [128, WCMAX], F32)
            qt1 = C.tile(esC, "qt1", [128, WCMAX], F32)
            qt2 = C.tile(esC, "qt2", [128, WCMAX], F32)
            qsq = C.tile(esC, "qsq", [128, WCMAX], BF16)
            ET = [C.tile(esC, "et%d" % i, [128, 2, WCMAX], BF16) for i in range(2)]
            aall = C.tile(esC, "aall", [128, 24, SC], BF16)
            TG = [C.tile(esC, "tg%d" % i, [128, SC], F32) for i in range(4)]
            TV = [C.tile(esC, "tv%d" % i, [128, SC], F32) for i in range(4)]
            FS = [C.tile(esC, "fs%d" % i, [128, SC], F32) for i in range(2)]
            SF = S[0]
            po, pden = P[0], P[1]
            BK = [(S[1], 0), (S[1], 1), (P[2], None), (P[3], None)]

            def bk(k, lo, hi):
                t, h = BK[k]
                return t[:, h, lo:hi] if h is not None else t[:, lo:hi]

            nitems = len(ITEMS)
            item_idx = {(k, j): n for n, (k, j, w) in enumerate(ITEMS)}
            st = {"issued": 0, "used": 0, "seq": [], "dry": True, "released": set()}

            def issue_upto(n):
                seq = st["seq"]
                while st["issued"] < min(n, len(seq)):
                    g = st["issued"]
                    if g >= NSLOT and (g - NSLOT) not in st["released"]:
                        break
                    it = ITEMS[seq[g]]
                    off = ITEM_OFF[seq[g]]
                    slot = ring[g % NSLOT]
                    dma("sp", slot[:, 0:it[2]], scr_d[:, off:off + it[2]], reads=[scr], writes=[slot], semtile=slot)
                    st["issued"] += 1

            def get_item(kind, j):
                if st["dry"]:
                    st["seq"].append(item_idx[(kind, j)])
                    return ring[0], None
                g = st["used"]
                assert st["seq"][g] == item_idx[(kind, j)]
                issue_upto(g + NSLOT)
                assert st["issued"] > g
                st["used"] += 1
                return ring[g % NSLOT], g

            def release(g):
                if g is None:
                    return
                st["released"].add(g)
                issue_upto(st["used"] + NSLOT)

            def dop(e, fn, reads=(), writes=()):
                if not st["dry"]:
                    op(e, fn, reads, writes)

            def ddma(*a, **k):
                if not st["dry"]:
                    dma(*a, **k)

            def win(i):
                s = i * SC
                e = min(s + SC, T)
                return s, e, e - s, e - s + 2

            def rms_stats(X, lo, hi, Pt, sq, sd, rs):
                n = hi - lo
                dop("act", lambda: A.activation(out=sq[:, :, 0:n], in_=X[:, :, lo:hi], func=AF.Square), reads=[X], writes=[sq])

                def mm():
                    for c in range(8):
                        last = PE.matmul(Pt[:, 0:n], lhsT=ones_bf, rhs=sq[:, c, 0:n], start=(c == 0), stop=(c == 7))
                    return last
                dop("pe", mm, reads=[sq, cbf], writes=[Pt])
                dop("act", lambda: A.activation(out=sd[:, 0:n], in_=Pt[:, 0:n], func=AF.Sqrt, scale=1.0 / D, bias=EPS),
                    reads=[Pt], writes=[sd])
                dop("dve", lambda: V.reciprocal(out=rs[:, 0:n], in_=sd[:, 0:n]), reads=[sd], writes=[rs])

            def front(i):
                s, e, Wc, W = win(i)
                X = xw[i % 2]
                attn = attn2[i % 2]
                ddma("sp", X[:, :, 0:W], xv[:, :, s + 1:e + 3], writes=[X])
                ddma("sp", tbw[:, :, 0:W], tab_d[:, :, s + 1:e + 3], writes=[tbw])
                rms_stats(X, 0, W, po, sqF, sdF, rsF)
                for c in range(8):
                    dop("pool", lambda c=c: G.tensor_scalar(out=xb[:, c, 0:W], in0=X[:, c, 0:W],
                                                             scalar1=pvec[:, PV_G1 + c:PV_G1 + c + 1], scalar2=0.0,
                                                             op0=ALU.mult, op1=ALU.add), reads=[X, pvec], writes=[xb])
                yield 6.0
                for j in range(4):
                    wt, gi = get_item("wq", j)
                    pq = None

                    def q_mm(wt=wt):
                        for kc in range(8):
                            last = PE.matmul(pden[:, 0:W], lhsT=wt[:, kc * 128:(kc + 1) * 128], rhs=xb[:, kc, 0:W],
                                             start=(kc == 0), stop=(kc == 7))
                        return last
                    dop("pe", q_mm, reads=[wt, xb], writes=[pden])
                    release(gi)
                    dop("dve", lambda: V.tensor_tensor(out=qz[:, 0:W], in0=pden[:, 0:W], in1=rsF[:, 0:W], op=ALU.mult),
                        reads=[pden, rsF], writes=[qz])
                    dop("act", lambda: A.activation(out=qsq[:, 0:W], in_=qz[:, 0:W], func=AF.Square), reads=[qz], writes=[qsq])
                    dop("pe", lambda: PE.matmul(po[:, 0:W], lhsT=bd_bf, rhs=qsq[:, 0:W], start=True, stop=True),
                        reads=[cbf, qsq], writes=[po])
                    dop("act", lambda: A.activation(out=qsd[:, 0:W], in_=po[:, 0:W], func=AF.Sqrt, scale=1.0 / 64, bias=EPS),
                        reads=[po], writes=[qsd])
                    dop("dve", lambda: V.reciprocal(out=qsd[:, 0:W], in_=qsd[:, 0:W]), reads=[qsd], writes=[qsd])
                    dop("dve", lambda: V.scalar_tensor_tensor(out=qn[:, 0:W], in0=qz[:, 0:W], scalar=pvec[:, PV_GQ:PV_GQ + 1],
                                                              in1=qsd[:, 0:W], op0=ALU.mult, op1=ALU.mult),
                        reads=[qz, qsd, pvec], writes=[qn])
                    dop("pe", lambda: PE.matmul(po[:, 0:W], lhsT=R32, rhs=qn[:, 0:W], start=True, stop=True),
                        reads=[cmat, qn], writes=[po])
                    dop("pool", lambda: G.tensor_tensor(out=qt1[:, 0:W], in0=qn[:, 0:W], in1=tbw[:, 0, 0:W], op=ALU.mult),
                        reads=[qn, tbw], writes=[qt1])
                    dop("dve", lambda: V.tensor_tensor(out=qt2[:, 0:W], in0=po[:, 0:W], in1=tbw[:, 1, 0:W], op=ALU.mult),
                        reads=[po, tbw], writes=[qt2])
                    dop("dve", lambda j=j: V.tensor_tensor(out=qT[:, j, 0:W], in0=qt1[:, 0:W], in1=qt2[:, 0:W], op=ALU.add),
                        reads=[qt1, qt2], writes=[qT])
                    yield 8.0
                for j in range(4):
                    def sc_mm(kt, j=j):
                        PE.matmul(SF[:, 0, 0:W], lhsT=kT[0:64, kt * 128:(kt + 1) * 128], rhs=qT[0:64, j, 0:W], start=True, stop=True)
                        return PE.matmul(SF[:, 1, 0:W], lhsT=kT[64:128, kt * 128:(kt + 1) * 128], rhs=qT[64:128, j, 0:W],
                                         start=True, stop=True)
                    dop("pe", lambda: sc_mm(0), reads=[kT, qT], writes=[SF])
                    for kt in range(32):
                        E = ET[kt % 2]
                        dop("act", lambda E=E: A.activation(out=E[:, :, 0:W], in_=SF[:, :, 0:W], func=AF.Exp, scale=0.125),
                            reads=[SF], writes=[E])

                        def pv_mm(kt=kt, E=E):
                            f = (kt == 0)
                            l = (kt == 31)
                            PE.matmul(po[0:64, 0:W], lhsT=vt[:, kt, 0:64], rhs=E[:, 0, 0:W], start=f, stop=l)
                            PE.matmul(po[64:128, 0:W], lhsT=vt[:, kt, 64:128], rhs=E[:, 1, 0:W], start=f, stop=l)
                            PE.matmul(pden[0:64, 0:W], lhsT=ones_bf[:, 0:64], rhs=E[:, 0, 0:W], start=f, stop=l)
                            return PE.matmul(pden[64:128, 0:W], lhsT=ones_bf[:, 64:128], rhs=E[:, 1, 0:W], start=f, stop=l)
                        if kt + 1 < 32:
                            dop("pe", lambda kt=kt: sc_mm(kt + 1), reads=[kT, qT], writes=[SF])
                        dop("pe", pv_mm, reads=[vt, E, cbf], writes=[po, pden])
                        yield 1.0
                    dop("dve", lambda: V.reciprocal(out=rden[:, 0:W], in_=pden[:, 0:W]), reads=[pden], writes=[rden])
                    dop("dve", lambda j=j: V.tensor_tensor(out=attn[:, j, 0:W], in0=po[:, 0:W], in1=rden[:, 0:W], op=ALU.mult),
                        reads=[po, rden], writes=[attn])
                    yield 1.0

            def back(i):
                s, e, Wc, W = win(i)
                X = xw[i % 2]
                attn = attn2[i % 2]

                def stageA(jj, kg, kv_, tg_t, tv_t):
                    for (k, Bt, dstt, ch) in ((kg, tg_t, TG[jj % 4], jj), (kv_, tv_t, TV[jj % 4], 24 + jj)):
                        w0c = PV_UCW + 0 * 48 + ch
                        w1c = PV_UCW + 1 * 48 + ch
                        w2c = PV_UCW + 2 * 48 + ch
                        bc = PV_UCB + ch
                        dop("dve", lambda k=k, dstt=dstt, w1c=w1c, bc=bc: V.tensor_scalar(
                            out=dstt[:, 0:Wc], in0=bk(k, 1, 1 + Wc), scalar1=pvec[:, w1c:w1c + 1], scalar2=pvec[:, bc:bc + 1],
                            op0=ALU.mult, op1=ALU.add), reads=[Bt, pvec], writes=[dstt])
                        dop("dve", lambda k=k, dstt=dstt, w0c=w0c: V.scalar_tensor_tensor(
                            out=dstt[:, 0:Wc], in0=bk(k, 0, Wc), scalar=pvec[:, w0c:w0c + 1], in1=dstt[:, 0:Wc],
                            op0=ALU.mult, op1=ALU.add), reads=[Bt, pvec, dstt], writes=[dstt])
                        dop("dve", lambda k=k, dstt=dstt, w2c=w2c: V.scalar_tensor_tensor(
                            out=dstt[:, 0:Wc], in0=bk(k, 2, 2 + Wc), scalar=pvec[:, w2c:w2c + 1], in1=dstt[:, 0:Wc],
                            op0=ALU.mult, op1=ALU.add), reads=[Bt, pvec, dstt], writes=[dstt])

                def stageB(jj):
                    tg, fs = TG[jj % 4], FS[jj % 2]
                    dop("pool", lambda: G.tensor_tensor(out=fs[:, 0:Wc], in0=tg[:, 0:Wc], in1=tg[:, 0:Wc], op=ALU.mult),
                        reads=[tg], writes=[fs])
                    dop("pool", lambda: G.tensor_scalar(out=fs[:, 0:Wc], in0=fs[:, 0:Wc], scalar1=0.044715, scalar2=1.0,
                                                         op0=ALU.mult, op1=ALU.add), reads=[fs], writes=[fs])
                    dop("pool", lambda: G.tensor_tensor(out=fs[:, 0:Wc], in0=fs[:, 0:Wc], in1=tg[:, 0:Wc], op=ALU.mult),
                        reads=[fs, tg], writes=[fs])

                def stageC(jj):
                    tg, tv, fs = TG[jj % 4], TV[jj % 4], FS[jj % 2]
                    dop("act", lambda: A.activation(out=fs[:, 0:Wc], in_=fs[:, 0:Wc], func=AF.Tanh, scale=GC),
                        reads=[fs], writes=[fs])
                    dop("dve", lambda: V.scalar_tensor_tensor(out=tg[:, 0:Wc], in0=fs[:, 0:Wc], scalar=1.0, in1=tg[:, 0:Wc],
                                                              op0=ALU.add, op1=ALU.mult), reads=[fs, tg], writes=[tg])
                    dop("dve", lambda: V.scalar_tensor_tensor(out=aall[:, jj, 0:Wc], in0=tg[:, 0:Wc], scalar=0.5,
                                                              in1=tv[:, 0:Wc], op0=ALU.mult, op1=ALU.mult),
                        reads=[tg, tv], writes=[aall])
                for m in range(8):
                    wt, gi = get_item("wo", m)
                    k = m % 4
                    pt = BK[k][0]
                    for g in range(2):
                        def o_mm(wt=wt, k=k, g=g):
                            for rc in range(g * 4, g * 4 + 4):
                                rhs = attn[:, rc, 0:W] if rc < 4 else lru[:, rc - 4, s:s + W]
                                last = PE.matmul(bk(k, 0, W), lhsT=wt[:, rc * 128:(rc + 1) * 128], rhs=rhs, start=(rc == 0), stop=(rc == 7))
                            return last
                        dop("pe", o_mm, reads=[wt, attn, lru], writes=[pt])
                        if g == 1:
                            release(gi)
                        yield 1.0
                    dop("dve", lambda m=m, k=k: V.tensor_tensor(out=X[:, m, 0:W], in0=bk(k, 0, W), in1=X[:, m, 0:W], op=ALU.add),
                        reads=[pt, X], writes=[X])
                rms_stats(X, 0, W, P[3], sqB, sdB, rsB)
                for m in range(8):
                    dop("dve", lambda m=m: V.scalar_tensor_tensor(out=h2[:, m, 0:W], in0=X[:, m, 0:W],
                                                                  scalar=pvec[:, PV_G2 + m:PV_G2 + m + 1], in1=rsB[:, 0:W],
                                                                  op0=ALU.mult, op1=ALU.mult), reads=[X, rsB, pvec], writes=[h2])
                if i == 0:
                    dop("pool", lambda: G.memset(h2[:, :, 0:1], 0.0), writes=[h2])
                if e == T:
                    dop("pool", lambda: G.memset(h2[:, :, W - 1:W], 0.0), writes=[h2])
                yield 3.0
                for jj in range(24):
                    wt, gi = get_item("wu", jj)
                    kg, kv_ = (0, 1) if jj % 2 == 0 else (2, 3)
                    tg_t, tv_t = BK[kg][0], BK[kv_][0]
                    for g in range(4):
                        def u_mm(wt=wt, g=g, kg=kg, kv_=kv_):
                            half = g // 2
                            k = kg if half == 0 else kv_
                            for kc in range((g % 2) * 4, (g % 2) * 4 + 4):
                                last = PE.matmul(bk(k, 0, W), lhsT=wt[:, half * 1024 + kc * 128:half * 1024 + (kc + 1) * 128],
                                                 rhs=h2[:, kc, 0:W], start=(kc == 0), stop=(kc == 7))
                            return last
                        dop("pe", u_mm, reads=[wt, h2], writes=[tg_t if g < 2 else tv_t])
                        if g == 3:
                            release(gi)
                        yield 1.0
                    if jj >= 2:
                        stageC(jj - 2)
                    if jj >= 1:
                        stageB(jj - 1)
                    stageA(jj, kg, kv_, tg_t, tv_t)
                stageB(23)
                stageC(22)
                stageC(23)
                for m in range(8):
                    wt, gi = get_item("wd", m)
                    k = m % 4
                    pt = BK[k][0]
                    for g in range(6):
                        def d_mm(wt=wt, k=k, g=g):
                            for jc in range(g * 4, g * 4 + 4):
                                last = PE.matmul(bk(k, 0, Wc), lhsT=wt[:, jc * 128:(jc + 1) * 128], rhs=aall[:, jc, 0:Wc],
                                                 start=(jc == 0), stop=(jc == 23))
                            return last
                        dop("pe", d_mm, reads=[wt, aall], writes=[pt])
                        if g == 5:
                            release(gi)
                        yield 1.0
                    dop("dve", lambda m=m, k=k: V.tensor_tensor(out=X[:, m, 1:1 + Wc], in0=bk(k, 0, Wc), in1=X[:, m, 1:1 + Wc],
                                                                op=ALU.add), reads=[pt, X], writes=[X])
                rms_stats(X, 1, 1 + Wc, P[3], sqB, sdB, rsB)
                for m in range(8):
                    dop("dve", lambda m=m: V.scalar_tensor_tensor(out=X[:, m, 1:1 + Wc], in0=X[:, m, 1:1 + Wc],
                                                                  scalar=pvec[:, PV_GF + m:PV_GF + m + 1], in1=rsB[:, 0:Wc],
                                                                  op0=ALU.mult, op1=ALU.mult), reads=[X, rsB, pvec], writes=[X])
                ddma("sp", ov[:, :, s:e], X[:, :, 1:1 + Wc], reads=[X], semtile=X)
                yield 3.0

            def schedule():
                def run_pair(ga, gb):
                    ta = tb_ = 0.0
                    da = ga is None
                    db = gb is None
                    while not (da and db):
                        if not da and (db or ta <= tb_):
                            try:
                                ta += next(ga)
                            except StopIteration:
                                da = True
                        else:
                            try:
                                tb_ += next(gb)
                            except StopIteration:
                                db = True
                run_pair(front(0), None)
                for i in range(nwc):
                    run_pair(front(i + 1) if i + 1 < nwc else None, back(i))

            if nwc > 0:
                st["dry"] = True
                schedule()
                st["dry"] = False
                issue_upto(NSLOT)
                schedule()
            C.barrier(dma_tiles=xw)
    return nc, dbg


def _rope_tables():
    half = 32
    inv_freq = (1.0 / (np.float32(10000.0) ** (np.arange(0, half, 2, dtype=np.float32) / np.float32(half)))).astype(np.float32)
    t = np.arange(T)
    row = (t // 64).astype(np.float32)
    col = (t % 64).astype(np.float32)
    tab = np.zeros((128, 2, T + 4), np.float32)
    for p in range(128):
        d = p % 64
        pos = row if d < 32 else col
        w = d % 32
        fi = w % 16
        ang = (pos * inv_freq[fi]).astype(np.float32)
        tab[p, 0, 2:T + 2] = np.cos(ang)
        sn = np.sin(ang)
        tab[p, 1, 2:T + 2] = -sn if w < 16 else sn
    return tab


def _consts():
    cm = np.zeros((128, 512), np.float32)
    cm[:, 0:128] = 1.0
    cm[0:64, 128:192] = 1.0
    cm[64:128, 192:256] = 1.0
    for m in range(128):
        w = (m % 64) % 32
        partner = m + 16 if w < 16 else m - 16
        cm[partner, 256 + m] = 1.0
    cm[:, 384:512] = np.eye(128, dtype=np.float32)
    return cm


def _host_prep(inp):
    f = lambda a: np.ascontiguousarray(np.asarray(a, dtype=np.float32))
    perm = []
    for j in range(4):
        perm += list(range(j * 64, (j + 1) * 64)) + list(range((4 + j) * 64, (5 + j) * 64))
    perm = np.array(perm)
    w_in = f(inp["w_in"])[0]
    wq = np.ascontiguousarray(w_in[:, 0:512][:, perm])
    wkvl = np.ascontiguousarray(w_in[:, 512:1792])
    w_out = f(inp["w_out"])[0]
    wo = np.ascontiguousarray(np.concatenate([w_out[0:512][perm], w_out[512:]], axis=0))
    wu = f(inp["w_up"])[0]
    wd = f(inp["w_down"])[0]
    pv = np.zeros((128, PV_N), np.float32)

    def chunked(v, n):
        return np.asarray(v, np.float32).reshape(n, 128).T
    pv[:, PV_G1:PV_G1 + 8] = chunked(inp["norm1_g"][0], 8)
    pv[:, PV_G2:PV_G2 + 8] = chunked(inp["norm2_g"][0], 8)
    pv[:, PV_GF:PV_GF + 8] = chunked(inp["final_g"], 8)
    pv[:, PV_GQ] = np.tile(np.asarray(inp["q_norm_g"][0], np.float32), 2)
    pv[:, PV_GK] = np.tile(np.asarray(inp["k_norm_g"][0], np.float32), 2)
    lcw = np.asarray(inp["lru_conv_w"][0], np.float32)
    for c in range(4):
        for j in range(4):
            pv[:, PV_LCW + c * 4 + j] = lcw[j, c * 128:(c + 1) * 128]
    pv[:, PV_LCB:PV_LCB + 4] = chunked(inp["lru_conv_b"][0], 4)
    pv[:, PV_BAF:PV_BAF + 4] = chunked(np.asarray(inp["ba_f"][0]).reshape(-1), 4)
    pv[:, PV_BXF:PV_BXF + 4] = chunked(np.asarray(inp["bx_f"][0]).reshape(-1), 4)
    pv[:, PV_BAB:PV_BAB + 4] = chunked(np.asarray(inp["ba_b"][0]).reshape(-1), 4)
    pv[:, PV_BXB:PV_BXB + 4] = chunked(np.asarray(inp["bx_b"][0]).reshape(-1), 4)
    pv[:, PV_LAMF:PV_LAMF + 4] = chunked(inp["lam_f"][0], 4)
    pv[:, PV_LAMB:PV_LAMB + 4] = chunked(inp["lam_b"][0], 4)
    ucw = np.asarray(inp["up_conv_w"][0], np.float32)
    for tap in range(3):
        pv[:, PV_UCW + tap * 48:PV_UCW + (tap + 1) * 48] = chunked(ucw[tap], 48)
    pv[:, PV_UCB:PV_UCB + 48] = chunked(inp["up_conv_b"][0], 48)
    gate = np.zeros((128, 16 * 128), np.float32)
    for kind, name in enumerate(("wa_f", "wx_f", "wa_b", "wx_b")):
        w = np.asarray(inp[name][0], np.float32)
        for c in range(4):
            base = (kind * 4 + c) * 128
            gate[0:64, base:base + 64] = w[2 * c]
            gate[64:128, base + 64:base + 128] = w[2 * c + 1]
    shared = {"wq": wq, "wkvl": wkvl, "wo": wo, "wu": wu, "wd": wd, "pvec": pv, "gatew": gate,
              "cmat": _consts(), "tab": _rope_tables()}
    x = np.asarray(inp["x"], np.float32)
    maps = []
    for b in range(x.shape[0]):
        xT = np.zeros((D, T + 4), np.float32)
        xT[:, 2:T + 2] = x[b].T
        m = dict(shared)
        m["xT"] = xT
        maps.append(m)
    return maps


_CACHE = {}


def kernel(**inputs):
    maps = _host_prep(inputs)
    if "nc" not in _CACHE:
        _CACHE["nc"] = _build()[0]
    nc = _CACHE["nc"]
    res = run_bass_kernel_spmd(nc, maps, core_ids=list(range(len(maps))))
    out = np.stack([np.ascontiguousarray(r["outT"].T) for r in res.results], axis=0)
    return out.astype(np.float32)
```
